# Optimizing a Trainium2 kernel written in Bass

```python
import jax, jax.numpy as jnp
from jax import lax
import numpy as np


D_MODEL = 1024
BATCH = 2
SEQ = 8192
DEPTH = 2

N_META = 16
EPS = 1e-6
POOL_WINDOWS = (2, 4, 8, 16)
N_POOL_GROUPS = 4
D_POOL = D_MODEL // 2
POOL_GROUP = D_POOL // N_POOL_GROUPS
N_HEADS = 8
D_LAT = D_MODEL // 8
N_IDX_HEADS = 8
D_IDX = 32
INDEX_SCALE = (N_IDX_HEADS ** -0.5) * (D_IDX ** -0.5)
ATTN_SCALE = D_LAT ** -0.5
Q_BLOCK = 128
TOP_K_MAX = 256
D_CONV = D_MODEL // 2
CONV_WIDTH = 3
N_BRANCH = 3
D_FF = -(-8 * D_MODEL // (3 * 256)) * 256
SPLITS = (D_POOL, N_HEADS * D_LAT, D_LAT, N_IDX_HEADS * D_IDX, D_IDX,
          N_IDX_HEADS, D_CONV, D_CONV, D_CONV, N_BRANCH * D_MODEL)
N_IN = sum(SPLITS)

kernel_name = "hybrid_pool_dsa_conv_gated_block"


def rms_norm(x, g):
    xf = x.astype(jnp.float32)
    y = xf * lax.rsqrt(jnp.mean(xf * xf, axis=-1, keepdims=True) + EPS)
    return (y * g.astype(jnp.float32)).astype(x.dtype)


def split_columns(p):
    offs, acc = [], 0
    for s in SPLITS[:-1]:
        acc += s
        offs.append(acc)
    return jnp.split(p, offs, axis=-1)


def pool_mixer(u, pool_w, pool_scale):
    B, L, _ = u.shape
    uf = u.astype(jnp.float32)
    cs = lax.cumsum(uf, axis=1)
    pos1 = jnp.arange(1, L + 1)
    outs = []
    for g, w in enumerate(POOL_WINDOWS):
        csg = cs[..., g * POOL_GROUP:(g + 1) * POOL_GROUP]
        shifted = jnp.pad(csg, ((0, 0), (w, 0), (0, 0)))[:, :L]
        count = jnp.minimum(pos1, w).astype(jnp.float32)[None, :, None]
        mean = (csg - shifted) / count
        outs.append(mean - uf[..., g * POOL_GROUP:(g + 1) * POOL_GROUP])
    pooled = jnp.stack(outs, axis=2).astype(u.dtype)
    mixed = jnp.einsum('blgc,gcd->blgd', pooled, pool_w).reshape(B, L, D_POOL)
    return mixed * pool_scale


def dsa_attention(q, qi, wi, c, ki):
    B, L = q.shape[0], q.shape[1]
    k_top = min(TOP_K_MAX, SEQ // 4)
    nb = -(-L // Q_BLOCK)
    Lp = nb * Q_BLOCK
    pad = Lp - L

    def to_blocks(a):
        a = jnp.pad(a, [(0, 0), (0, pad)] + [(0, 0)] * (a.ndim - 2))
        return jnp.moveaxis(a.reshape(B, nb, Q_BLOCK, *a.shape[2:]), 1, 0)

    kpos = jnp.arange(L, dtype=jnp.int32)
    qpos = jnp.arange(Lp, dtype=jnp.int32).reshape(nb, Q_BLOCK)
    wi = wi * INDEX_SCALE

    def block(args):
        qb, qib, wb, pb = args
        s = jnp.einsum('bqhd,bkd->bqhk', qib, ki).astype(jnp.float32)
        score = jnp.einsum('bqhk,bqh->bqk', jax.nn.relu(s), wb.astype(jnp.float32))
        causal = kpos[None, None, :] <= pb[None, :, None]
        score = jnp.where(causal, score, -jnp.inf)
        _, sel = lax.top_k(score, k_top)
        c_sel = jax.vmap(lambda cb, ib: cb[ib])(c, sel)
        logits = jnp.einsum('bqhd,bqkd->bqhk', qb, c_sel).astype(jnp.float32) * ATTN_SCALE
        valid = (sel <= pb[None, :, None])[:, :, None, :]
        p = jax.nn.softmax(jnp.where(valid, logits, -jnp.inf), axis=-1).astype(c.dtype)
        return jnp.einsum('bqhk,bqkd->bqhd', p, c_sel)

    out = lax.map(block, (to_blocks(q), to_blocks(qi), to_blocks(wi), qpos))
    out = jnp.moveaxis(out, 0, 1).reshape(B, Lp, N_HEADS * D_LAT)
    return out[:, :L]


def short_conv_mixer(b_gate, c_gate, u, conv_w):
    v = c_gate * u
    rhs = conv_w.reshape(CONV_WIDTH, 1, D_CONV)
    y = lax.conv_general_dilated(v, rhs, window_strides=(1,),
                                 padding=[(CONV_WIDTH - 1, 0)],
                                 dimension_numbers=('NWC', 'WIO', 'NWC'),
                                 feature_group_count=D_CONV)
    return b_gate * y


def hybrid_layer(h, norm1_g, w_in, b_gate, pool_w, pool_scale, w_a, kv_norm_g,
                 w_b, conv_w, w_c, w_o, norm2_g, w_gate_up, w_down):
    B, L, _ = h.shape
    xn = rms_norm(h, norm1_g)
    p = xn @ w_in
    (u_pool, q, c_lat, q_idx, k_idx, w_idx, cb, cc, cu, g_raw) = split_columns(p)
    y_a = pool_mixer(u_pool, pool_w, pool_scale) @ w_a
    c_lat = rms_norm(c_lat, kv_norm_g)
    attn = dsa_attention(q.reshape(B, L, N_HEADS, D_LAT),
                         q_idx.reshape(B, L, N_IDX_HEADS, D_IDX),
                         w_idx, c_lat, k_idx)
    y_b = attn @ w_b
    y_c = short_conv_mixer(cb, cc, cu, conv_w) @ w_c
    g = jax.nn.sigmoid(g_raw + b_gate).reshape(B, L, N_BRANCH, D_MODEL)
    m = g[:, :, 0] * y_a + g[:, :, 1] * y_b + g[:, :, 2] * y_c
    h = h + m @ w_o
    hn = rms_norm(h, norm2_g)
    gate, up = jnp.split(hn @ w_gate_up, 2, axis=-1)
    return h + (jax.nn.silu(gate) * up) @ w_down


def setup_inputs(seed: int = 0) -> dict:
    key = jax.random.key(seed)
    ks = jax.random.split(key, 20)
    f32 = jnp.float32

    def nrm(k, shape, fan):
        return jax.random.normal(k, shape, f32) * (fan ** -0.5)

    def gain(k, shape):
        return 1.0 + 0.1 * jax.random.normal(k, shape, f32)

    return {
        "x": jax.random.normal(ks[0], (BATCH, SEQ, D_MODEL), f32),
        "meta": jax.random.normal(ks[1], (N_META, D_MODEL), f32),
        "norm1_g": gain(ks[2], (DEPTH, D_MODEL)),
        "w_in": nrm(ks[3], (DEPTH, D_MODEL, N_IN), D_MODEL),
        "b_gate": 0.01 * jax.random.normal(ks[4], (DEPTH, N_BRANCH * D_MODEL), f32),
        "pool_w": nrm(ks[5], (DEPTH, N_POOL_GROUPS, POOL_GROUP, POOL_GROUP), POOL_GROUP),
        "pool_scale": gain(ks[6], (DEPTH, D_POOL)),
        "w_a": nrm(ks[7], (DEPTH, D_POOL, D_MODEL), D_POOL),
        "kv_norm_g": gain(ks[8], (DEPTH, D_LAT)),
        "w_b": nrm(ks[9], (DEPTH, N_HEADS * D_LAT, D_MODEL), N_HEADS * D_LAT),
        "conv_w": nrm(ks[10], (DEPTH, CONV_WIDTH, D_CONV), CONV_WIDTH),
        "w_c": nrm(ks[11], (DEPTH, D_CONV, D_MODEL), D_CONV),
        "w_o": nrm(ks[12], (DEPTH, D_MODEL, D_MODEL), D_MODEL),
        "norm2_g": gain(ks[13], (DEPTH, D_MODEL)),
        "w_gate_up": nrm(ks[14], (DEPTH, D_MODEL, 2 * D_FF), D_MODEL),
        "w_down": nrm(ks[15], (DEPTH, D_FF, D_MODEL), D_FF),
        "final_g": gain(ks[16], (D_MODEL,)),
    }


def reference(x, meta, norm1_g, w_in, b_gate, pool_w, pool_scale, w_a, kv_norm_g,
              w_b, conv_w, w_c, w_o, norm2_g, w_gate_up, w_down, final_g):
    B = x.shape[0]
    meta_b = jnp.broadcast_to(meta[None], (B, N_META, D_MODEL)).astype(x.dtype)
    h = jnp.concatenate([meta_b, x], axis=1)
    for l in range(DEPTH):
        h = hybrid_layer(h, norm1_g[l], w_in[l], b_gate[l], pool_w[l], pool_scale[l],
                         w_a[l], kv_norm_g[l], w_b[l], conv_w[l], w_c[l], w_o[l],
                         norm2_g[l], w_gate_up[l], w_down[l])
    return rms_norm(h, final_g)[:, N_META:]
```

```python
import numpy as np
from contextlib import ExitStack
import concourse.bass as bass
import concourse.mybir as mybir
from concourse.bass_utils import run_bass_kernel_spmd

F32 = mybir.dt.float32
BF16 = mybir.dt.bfloat16
U8 = mybir.dt.uint8
ALU = mybir.AluOpType
AF = mybir.ActivationFunctionType
AX = mybir.AxisListType

D = 1024
SEQ = 8192
NMETA = 16
LTOT = SEQ + NMETA
HALO = 32
OWN = 2048
CW = 544
NT = 4 * CW
KVW = 528
CSTW = 512 + 128 + 20 + 128 + 256
DEPTH = 2
EPS = 1e-6
N_IN = 6568
U0, Q0, C0, QI0, KI0, WI0, CB0, CC0, CU0, G0 = 0, 512, 1536, 1664, 1920, 1952, 1960, 2472, 2984, 3496
DFF = 2816
INDEX_SCALE = (8 ** -0.5) * (32 ** -0.5)
ATTN_SCALE = 128 ** -0.5
TOPK = 256
NEG = -2.0e4
MASKNEG = -30000.0
BIS_R = 16.0
BIS_IT = 16
TER_IT = 11
QUART = LTOT // 3
NGRP = 3
NSMALL = 72

CHUNKS = [(CW * i, CW, [(0, 32), (32, 512)]) for i in range(4)]
NMAX = 544
SUBS_ALL = [s_ for i in range(4) for s_ in ((CW * i, 32), (CW * i + 32, 512))]


def qblocks_of_chunk(ci):
    keh = NMETA + 512 * (4 * ci + 3)
    return [(0, 32, keh)] + [(32 + 128 * b_, 128, keh + 128 * (b_ + 1)) for b_ in range(4)]


def key_groups(KE):
    gs = []
    for g in range(NGRP):
        q0 = g * QUART
        q1 = min((g + 1) * QUART, KE)
        k = q0
        while k < q1:
            kn = min(512, q1 - k)
            gs.append((g, k - q0, k, kn))
            k += kn
    return gs


ALL_GROUPS = key_groups(LTOT)
NG = len(ALL_GROUPS)


nc_counts = {}

WSPEC = {
    "w_in": (1024, N_IN, 1024), "pool_w": (512, 128, 128), "w_a": (512, 1024, 1024), "w_b": (1024, 1024, 1024),
    "w_c": (512, 1024, 1024), "w_o": (1024, 1024, 1024), "w_gu": (1024, 2 * 2816, 1024), "w_dn": (2816, 1024, 256),
}
GROUPS4 = [[0, 1, 2, 3], [4, 5, 6, 7]]


def wpieces(name):
    K, N, pc = WSPEC[name]
    return [(c0, min(pc, N - c0)) for c0 in range(0, N, pc)]


class WMat:
    def __init__(self, K, pieces):
        self.K = K
        self.pieces = pieces

    def segs(self, col0, ncols, r0=0, kc=None):
        out = []
        kcc = self.K // 128 - r0 if kc is None else kc
        for (c0, pc, ap, key) in self.pieces:
            lo, hi = max(col0, c0), min(col0 + ncols, c0 + pc)
            if lo < hi:
                v = ap.rearrange("(kc p) n -> p kc n", p=128)
                out.append((lo - col0, hi - lo, v[:, r0:r0 + kcc, lo - c0:hi - c0], key))
        return out

    def seg1(self, col0, ncols):
        sg = self.segs(col0, ncols)
        assert len(sg) == 1
        return sg[0][2], sg[0][3]


class Prog:
    def __init__(self, nc, es):
        self.nc = nc
        self.es = es
        self.engs = {"pe": nc.tensor, "act": nc.scalar, "dve": nc.vector, "pool": nc.gpsimd, "sp": nc.sync}
        self.sems = {}
        self.cnt = {}
        self.isdma = {}
        self.waited = {}
        self.lastw = {}
        self.readers = {}
        for e in ("pe", "act", "dve", "pool"):
            self.new_sem(e, False)

    def new_sem(self, name, dma=True):
        self.sems[name] = self.es.enter_context(self.nc.semaphore("s_" + name))
        self.cnt[name] = 0
        self.isdma[name] = dma

    def _collect(self, r, w):
        deps = {}

        def add(tok):
            if tok is None:
                return
            s, v = tok
            if deps.get(s, 0) < v:
                deps[s] = v

        for k in r:
            add(self.lastw.get(k))
        for k in w:
            add(self.lastw.get(k))
            for s, v in self.readers.get(k, {}).items():
                add((s, v))
        return deps

    def _wait(self, e, deps):
        eng = self.engs[e]
        for s, v in deps.items():
            if self.isdma[s]:
                v = self.cnt[s]
            if self.waited.get((e, s), 0) < v:
                eng.wait_ge(self.sems[s], v)
                self.waited[(e, s)] = v

    def _record(self, r, w, tok):
        s, v = tok
        for k in r:
            d = self.readers.setdefault(k, {})
            if d.get(s, 0) < v:
                d[s] = v
        for k in w:
            self.lastw[k] = tok
            self.readers[k] = {}

    def op(self, e, fn, r=(), w=()):
        deps = self._collect(r, w)
        if e == "pe":
            deps.pop("pe", None)
        self._wait(e, deps)
        ins = fn(self.engs[e])
        self.cnt[e] += 1
        ins.then_inc(self.sems[e], 1)
        self._record(r, w, (e, self.cnt[e]))

    def dma(self, q, sem, fn, r=(), w=()):
        deps = self._collect(r, w)
        self._wait(q, deps)
        ins = fn(self.engs[q])
        self.cnt[sem] += 16
        ins.then_inc(self.sems[sem], 16)
        self._record(r, w, (sem, self.cnt[sem]))

    def barrier(self, engines=("pe", "act", "dve")):
        for e in engines:
            deps = {s: self.cnt[s] for s in ("pe", "act", "dve") if s != e and self.cnt[s] > 0}
            self._wait(e, deps)

    def wait_all(self, e, sems):
        self._wait(e, {s: self.cnt[s] for s in sems if self.cnt[s] > 0})


class _Stop(Exception):
    pass


def build(mode, dbg=False, stop=None):
    nc = bass.Bass("TRN2", target_bir_lowering=False)
    fused = mode == "FUSED"
    do_B = mode in ("L0", "L1")
    do_A = mode in ("A0", "L0")
    do_final = mode in ("L1", "FUSED")

    def din(name, shape):
        return nc.dram_tensor(name, shape, F32, kind="ExternalInput").ap()

    def dout(name, shape):
        return nc.dram_tensor(name, shape, F32, kind="ExternalOutput").ap()

    hin = din("hin", [D, NT])
    cst = din("cst", [128, CSTW + NG])
    small = din("small", [128, NSMALL])
    iota_in = nc.dram_tensor("iota16", [128, 512], mybir.dt.int16, kind="ExternalInput").ap()
    WL = [None, None]
    keys = None
    if fused:
        smallA = din("smallA", [128, NSMALL])
        shard_in, piece_in, piece_out, piece_bf = {}, {}, {}, {}
        for l in range(2):
            WL[l] = {}
            for name, (K, N, pc) in WSPEC.items():
                shard_in[(name, l)] = din("ws_%s_%d" % (name, l), [K // 4, N])
                pcs = []
                for pi, (c0, w_) in enumerate(wpieces(name)):
                    piece_in[(name, l, pi)] = nc.dram_tensor("pi_%s_%d_%d" % (name, l, pi), [K // 4, w_], F32)
                    piece_out[(name, l, pi)] = nc.dram_tensor("pg_%s_%d_%d" % (name, l, pi), [K, w_], F32)
                    piece_bf[(name, l, pi)] = nc.dram_tensor("pb_%s_%d_%d" % (name, l, pi), [K, w_], BF16)
                    pcs.append((c0, w_, piece_bf[(name, l, pi)].ap(), "pb_%s_%d_%d" % (name, l, pi)))
                WL[l][name] = WMat(K, pcs)
        kv_in = nc.dram_tensor("kv_in", [160, 4 * KVW], BF16)
        kv_all = nc.dram_tensor("kv_all", [640, 4 * KVW], BF16)
    else:
        lay = {"A0": 0, "L0": 0, "L1": 1}[mode]
        names = ["w_in"] if mode == "A0" else list(WSPEC.keys())
        WL[lay] = {nm: WMat(WSPEC[nm][0], [(0, WSPEC[nm][1], din(nm, [WSPEC[nm][0], WSPEC[nm][1]]), "ext")]) for nm in names}
        if mode == "L0":
            smallA = din("smallA", [128, NSMALL])
            WL[1] = {"w_in": WMat(1024, [(0, N_IN, din("w_inA", [D, N_IN]), "ext")])}
        elif mode == "A0":
            smallA = small
        if do_B:
            keys = din("keys", [160, LTOT])
        if do_A:
            kvout = dout("kvout", [160, NT])
    if mode == "L0":
        hout = dout("hout", [D, NT])
    if do_final:
        outT = dout("outT", [D, OWN])
    if dbg:
        dbg_attn = dout("dbg_attn", [D, NT]).rearrange("(k p) t -> p k t", p=128)
        dbg_m = dout("dbg_m", [D, NT]).rearrange("(k p) t -> p k t", p=128)
        dbg_hmid = dout("dbg_hmid", [D, NT]).rearrange("(k p) t -> p k t", p=128)

    with ExitStack() as es:
        P = Prog(nc, es)
        for s in ("ld", "ldp", "kv", "keys", "out", "w0", "w1", "w2", "ws", "dbgp", "bn0", "bn1", "bn2", "bn3", "bn4", "bn5", "cv0", "cv1", "cv2", "cv3", "cv4", "cv5"):
            P.new_sem(s)

        def sb(name, shape, dt):
            return es.enter_context(nc.sbuf_tensor(name, shape, dt))

        hT = sb("hT", [128, 8, NT], F32)
        cT = sb("cT", [128, LTOT], BF16)
        ki4 = sb("ki4", [128, QUART], BF16)
        attnT = sb("attnT", [128, 8, NMAX], BF16)
        cstt = sb("cstt", [128, CSTW - 640 + NG], F32)
        iota = sb("iota_sb", [128, 512], mybir.dt.int16)
        smt = sb("smt", [128, NSMALL], F32)
        smtA = sb("smtA", [128, NSMALL], F32)
        ones_bf = sb("ones_bf", [128, 128], BF16)
        I4 = sb("I4", [128, 512], BF16)
        ident_bf = I4[:, 0:128]
        wsm = sb("wsm", [128, 8, 168], BF16)
        wslots = [sb("wslot%d" % i, [128, 4096], BF16) for i in range(3)]
        tiny = sb("tiny", [128, 16], F32)
        junks = sb("junks", [128, 528], U8)
        junkA = sb("junkA", [128, 528], U8)
        wtok = sb("wtok", [128, 5, 8], F32)
        qrel = sb("qrel", [128, NG], F32)
        ARENA_W = 19736
        arena = sb("arena", [128, ARENA_W], F32)

        qpos = cstt[:, 0:20]
        cmask_all = cstt[:, 20:148]
        invcnt = cstt[:, 148:404]
        k0neg = cstt[:, CSTW - 640:CSTW - 640 + NG]

        psb = [es.enter_context(nc.psum_tensor("psb%d" % i, [128, 512], F32)) for i in range(7)]
        pst = es.enter_context(nc.psum_tensor("pst", [128, 1024], BF16))
        bank_rr = {"pool": [4, 5, 6], "i": 0}

        def next_bank():
            pool = bank_rr["pool"]
            b = pool[bank_rr["i"] % len(pool)]
            bank_rr["i"] += 1
            return b

        tr_rr = {"i": 0}

        def next_tr():
            i = tr_rr["i"] % 8
            tr_rr["i"] += 1
            return i

        def av_f32(off, shape):
            n = int(np.prod(shape))
            ap = arena[:, off:off + n]
            if len(shape) == 2:
                return ap.rearrange("p (a b) -> p a b", a=shape[0])
            return ap

        def av_bf(off, shape):
            n = int(np.prod(shape))
            assert n % 2 == 0
            ap = arena[:, off:off + n // 2].bitcast(BF16)
            if len(shape) == 2:
                return ap.rearrange("p (a b) -> p a b", a=shape[0])
            return ap

        wstate = {"i": 0}
        wq = "sp" if fused else "pool"

        def load_w(segs, kc, ncols):
            assert kc * ncols <= 4096
            i = wstate["i"] % 3
            wstate["i"] += 1
            view = wslots[i][:, 0:kc * ncols].rearrange("p (k n) -> p k n", k=kc)
            key = "wslot%d" % i
            for (off, n_, ap_, pkey) in segs:
                P.dma(wq, "w%d" % i, lambda e, off=off, n_=n_, ap_=ap_: e.dma_start(out=view[:, :, off:off + n_], in_=ap_), r=(pkey,), w=(key,))
            return view, key

        def wsrc(W, col0, ncols, r0=0, kc=None):
            return W.segs(col0, ncols, r0, kc)

        P.dma("sp", "ld", lambda e: e.dma_start(out=cstt[:], in_=cst[:, 640:CSTW + NG]), w=("cst",))
        P.dma("sp", "ld", lambda e: e.dma_start(out=iota[:], in_=iota_in), w=("cst",))
        for i4 in range(4):
            P.dma("pool", "ldp", lambda e, i4=i4: e.dma_start(out=I4[:, 128 * i4:128 * i4 + 128], in_=cst[:, 512:640]), w=("I4", "ident"))
        P.dma("sp", "ld", lambda e: e.dma_start(out=smt[:], in_=small), w=("small",))
        if do_A or fused:
            P.dma("sp", "ld", lambda e: e.dma_start(out=smtA[:], in_=smallA), w=("smallA",))
        hin_v = hin.rearrange("(k p) t -> p k t", p=128)
        for k in range(8):
            P.dma("sp", "ld", lambda e, k=k: e.dma_start(out=hT[:, k, :], in_=hin_v[:, k, :]), w=("h",))
        P.op("dve", lambda e: e.memset(ones_bf[:], 1.0), w=("ones",))

        def smv(t):
            return dict(g1=t[:, 0:8], g2=t[:, 8:16], kvg=t[:, 16:17], bg=t[:, 17:41], psc=t[:, 41:45],
                        cw=t[:, 45:57], fg=t[:, 57:65])

        SM = smv(smt)
        SMA = smv(smtA)

        def rms_norm_to(xn_view, t0, subs, gvec, gkey, sq_off):
            sq = av_bf(sq_off, (8, 512))
            rs = av_f32(sq_off + 2048, (512,))
            for (s0, n) in subs:
                b = next_bank()
                for k in range(8):
                    P.op("act", lambda e, k=k: e.activation(out=sq[:, k, 0:n], in_=hT[:, k, t0 + s0:t0 + s0 + n], func=AF.Square),
                         r=("h",), w=("sq%d" % k,))
                for k in range(8):
                    P.op("pe", lambda e, k=k: e.matmul(psb[b][:, 0:n], ones_bf[:], sq[:, k, 0:n], start=(k == 0), stop=(k == 7)),
                         r=("sq%d" % k, "ones"), w=("ps%d" % b,))
                P.op("act", lambda e: e.activation(out=rs[:, 0:n], in_=psb[b][:, 0:n], func=AF.Sqrt, scale=1.0 / D, bias=EPS),
                     r=("ps%d" % b,), w=("rs",))
                P.op("dve", lambda e: e.reciprocal(out=rs[:, 0:n], in_=rs[:, 0:n]), r=("rs",), w=("rs",))
                for k in range(8):
                    P.op("dve", lambda e, k=k: e.scalar_tensor_tensor(out=xn_view[:, k, s0:s0 + n], in0=hT[:, k, t0 + s0:t0 + s0 + n],
                                                                       scalar=gvec[:, k:k + 1], in1=rs[:, 0:n], op0=ALU.mult, op1=ALU.mult),
                         r=("h", "rs", gkey), w=("xn",))

        def stage_A(w_in_l, smx, smkey, subs_a=None, load=True):
            P.barrier()
            bank_rr["pool"] = [0, 1, 2, 3, 4, 5, 6]
            if subs_a is None:
                subs_a = SUBS_ALL
            if load:
                ap_c, k_c = w_in_l.seg1(C0, 128)
                ap_k, k_k = w_in_l.seg1(KI0, 32)
                P.dma(wq, "ws", lambda e: e.dma_start(out=wsm[:, :, 0:128], in_=ap_c), r=(k_c,), w=("wsm",))
                P.dma(wq, "ws", lambda e: e.dma_start(out=wsm[:, :, 128:160], in_=ap_k), r=(k_k,), w=("wsm",))
            xn = av_bf(0, (8, NMAX))
            c32 = av_f32(8192, (512,))
            csq = av_bf(8192 + 512, (512,))
            rs2 = av_f32(8192 + 1024, (512,))
            ki32 = av_f32(8192 + 1536, (512,))
            cbf = av_bf(8192 + 2048, (512,))
            kibf = av_bf(8192 + 2304, (512,))
            for (t0, n) in subs_a:
                rms_norm_to(xn, t0, [(0, n)], smx["g1"], smkey, 4352)
                b = next_bank()
                for k in range(8):
                    P.op("pe", lambda e, k=k: e.matmul(psb[b][:, 0:n], wsm[:, k, 0:128], xn[:, k, 0:n], start=(k == 0), stop=(k == 7)),
                         r=("xn", "wsm"), w=("ps%d" % b,))
                P.op("act", lambda e: e.activation(out=csq[:, 0:n], in_=psb[b][:, 0:n], func=AF.Square), r=("ps%d" % b,), w=("csq",))
                b2 = next_bank()
                P.op("pe", lambda e: e.matmul(psb[b2][:, 0:n], ones_bf[:], csq[:, 0:n], start=True, stop=True),
                     r=("csq", "ones"), w=("ps%d" % b2,))
                P.op("act", lambda e: e.activation(out=rs2[:, 0:n], in_=psb[b2][:, 0:n], func=AF.Sqrt, scale=1.0 / 128, bias=EPS),
                     r=("ps%d" % b2,), w=("rs2",))
                P.op("dve", lambda e: e.reciprocal(out=rs2[:, 0:n], in_=rs2[:, 0:n]), r=("rs2",), w=("rs2",))
                cdst = cbf if fused else c32
                kdst = kibf if fused else ki32
                P.op("dve", lambda e: e.scalar_tensor_tensor(out=cdst[:, 0:n], in0=psb[b][:, 0:n], scalar=smx["kvg"], in1=rs2[:, 0:n],
                                                             op0=ALU.mult, op1=ALU.mult), r=("ps%d" % b, "rs2", smkey), w=("c32",))
                b3 = next_bank()
                for k in range(8):
                    P.op("pe", lambda e, k=k: e.matmul(psb[b3][0:32, 0:n], wsm[:, k, 128:160], xn[:, k, 0:n], start=(k == 0), stop=(k == 7)),
                         r=("xn", "wsm"), w=("ps%d" % b3,))
                P.op("act", lambda e: e.copy(out=kdst[0:32, 0:n], in_=psb[b3][0:32, 0:n]), r=("ps%d" % b3,), w=("ki32",))
                if fused:
                    ich, cc_ = t0 // CW, t0 % CW
                    lo = max(cc_, 16)
                    d0, d1 = KVW * ich + lo - 16, KVW * ich + cc_ + n - 16
                    P.dma("sp", "kv", lambda e: e.dma_start(out=kv_in.ap()[0:128, d0:d1], in_=cdst[:, lo - cc_:n]), r=("c32",), w=("kvout",))
                    P.dma("sp", "kv", lambda e: e.dma_start(out=kv_in.ap()[128:160, d0:d1], in_=kdst[0:32, lo - cc_:n]), r=("ki32",), w=("kvout",))
                else:
                    P.dma("sp", "kv", lambda e: e.dma_start(out=kvout[0:128, t0:t0 + n], in_=c32[:, 0:n]), r=("c32",), w=("kvout",))
                    P.dma("sp", "kv", lambda e: e.dma_start(out=kvout[128:160, t0:t0 + n], in_=ki32[0:32, 0:n]), r=("ki32",), w=("kvout",))
            P.barrier()

        def stage_B(Wd, smx, smkey, hooks=None, skip_halo=False, post_chunk=None):
            w_in, pool_w, w_a, w_b, w_c, w_o, w_gu, w_dn = (Wd[k_] for k_ in ("w_in", "pool_w", "w_a", "w_b", "w_c", "w_o", "w_gu", "w_dn"))
            def load_keys():
              if fused:
                kva = kv_all.ap()

                def key_src(k0, k1):
                    segs = [(0, NMETA, 0, 0)] + [(NMETA + 512 * g_, 512, g_ % 4, KVW * (g_ // 4) + 16) for g_ in range(16)]
                    out = []
                    for (g0, gn, r, col) in segs:
                        lo, hi = max(k0, g0), min(k1, g0 + gn)
                        if lo < hi:
                            out.append((lo - k0, hi - lo, r, col + lo - g0))
                    return out

                for (off, n_, r, c0_) in key_src(0, LTOT):
                    P.dma("sp", "keys", lambda e, off=off, n_=n_, r=r, c0_=c0_: e.dma_start(out=cT[:, off:off + n_], in_=kva[160 * r:160 * r + 128, c0_:c0_ + n_]),
                          r=("kvall",), w=("cT",))
                for g in range(NGRP):
                    for (off, n_, r, c0_) in key_src(g * QUART, (g + 1) * QUART):
                        P.dma("sp", "keys", lambda e, off=off, n_=n_, r=r, c0_=c0_, g=g: e.dma_start(
                            out=ki4[32 * g:32 * g + 32, off:off + n_], in_=kva[160 * r + 128:160 * r + 160, c0_:c0_ + n_]), r=("kvall",), w=("ki4",))
              else:
                P.dma("pool", "keys", lambda e: e.dma_start(out=cT[:], in_=keys[0:128, :]), w=("cT",))
                for g in range(NGRP):
                    P.dma("pool", "keys", lambda e, g=g: e.dma_start(out=ki4[32 * g:32 * g + 32, :], in_=keys[128:160, g * QUART:(g + 1) * QUART]),
                          w=("ki4",))

            ap_w, k_w = w_in.seg1(WI0, 8)
            P.dma(wq, "ws", lambda e: e.dma_start(out=wsm[:, :, 160:168], in_=ap_w), r=(k_w,), w=("wsm",))

            for oi, ci in enumerate((3, 2, 1, 0)):
                t0, n, subs = CHUNKS[ci]
                qbs = qblocks_of_chunk(ci)
                cmask = cmask_all[:, 32 * ci:32 * ci + 32]
                P.barrier()
                bank_rr["pool"] = [0, 1, 2, 3, 4, 5, 6]
                xnA = av_bf(0, (8, NMAX))
                scores = av_f32(0, (LTOT,))
                q_t = av_bf(8208, (8, NMAX))
                qi_t = av_bf(10384, (8, NMAX))
                mbias = av_bf(12560, (LTOT,))
                rbufs = [av_f32(16664 + 512 * i, (512,)) for i in range(2)]
                pbufs = [av_bf(17688 + 512 * i, (8, 128)) for i in range(3)]
                cbats = [av_bf(19224 + 256 * i, (4, 128)) for i in range(2)]
                rcb = av_f32(17688, (512,))
                rms_norm_to(xnA, t0, subs, smx["g1"], smkey, 12560)
                for half in range(2):
                    wv, wk = load_w(wsrc(w_in, Q0 + 512 * half, 512), 8, 512)
                    for hh in range(4):
                        h = half * 4 + hh
                        for (s0, sn) in subs:
                            b = next_bank()
                            for k in range(8):
                                P.op("pe", lambda e, k=k: e.matmul(psb[b][:, 0:sn], wv[:, k, 128 * hh:128 * hh + 128], xnA[:, k, s0:s0 + sn],
                                                                     start=(k == 0), stop=(k == 7)), r=("xn", wk), w=("ps%d" % b,))
                            P.op("act", lambda e: e.copy(out=q_t[:, h, s0:s0 + sn], in_=psb[b][:, 0:sn]), r=("ps%d" % b,), w=("q",))
                for half in range(2):
                    i = wstate["i"] % 3
                    wstate["i"] += 1
                    key = "wslot%d" % i
                    view = wslots[i][:, 0:3072].rearrange("p (k h r d) -> p k h r d", k=8, h=4, r=3)
                    for hh in range(4):
                        h = half * 4 + hh
                        ap_q, k_q = w_in.seg1(QI0 + 32 * h, 32)
                        for r_ in range(3):
                            P.dma(wq, "w%d" % i, lambda e, hh=hh, r_=r_, ap_q=ap_q: e.dma_start(
                                out=view[:, :, hh, r_, :], in_=ap_q), r=(k_q,), w=(key,))
                    for hh in range(4):
                        h = half * 4 + hh
                        lw = wslots[i][:, 0:3072].rearrange("p (k h m) -> p k h m", k=8, h=4)
                        for (s0, sn) in subs:
                            b = next_bank()
                            for k in range(8):
                                P.op("pe", lambda e, k=k: e.matmul(psb[b][0:96, 0:sn], lw[:, k, hh, :], xnA[:, k, s0:s0 + sn],
                                                                     start=(k == 0), stop=(k == 7)), r=("xn", key), w=("ps%d" % b,))
                            P.op("act", lambda e: e.copy(out=qi_t[0:96, h, s0:s0 + sn], in_=psb[b][0:96, 0:sn]), r=("ps%d" % b,), w=("qi",))
                for qi_, (c0, nq, KE) in enumerate(qbs):
                    b = next_bank()
                    for k in range(8):
                        P.op("pe", lambda e, k=k: e.matmul(psb[b][0:nq, 0:8], xnA[:, k, c0:c0 + nq], wsm[:, k, 160:168],
                                                             start=(k == 0), stop=(k == 7)), r=("xn", "wsm"), w=("ps%d" % b,))
                    P.op("dve", lambda e: e.tensor_scalar(out=wtok[0:nq, qi_, :], in0=psb[b][0:nq, 0:8], scalar1=INDEX_SCALE, scalar2=None,
                                                          op0=ALU.mult), r=("ps%d" % b,), w=("wtok",))

                if oi == 0:
                    load_keys()
                if hooks and (oi, "p1") in hooks:
                    hooks[(oi, "p1")]()
                if stop == "p1":
                    raise _Stop()
                P.barrier()
                bank_rr["pool"] = [4, 5, 6]
                def idx_chain(qi_):
                    c0, nq, KE = qbs[qi_]
                    groups = key_groups(KE)
                    ng = len(groups)
                    qcol = 5 * ci + qi_
                    P.op("dve", lambda e: e.tensor_scalar(out=qrel[0:nq, 0:ng], in0=k0neg[0:nq, 0:ng], scalar1=qpos[0:nq, qcol:qcol + 1],
                                                          scalar2=None, op0=ALU.add), r=("cst",), w=("qrel",))
                    for gi, (g, j0, k0, kn) in enumerate(groups):
                        P.op("dve", lambda e: e.tensor_scalar(out=scores[0:nq, k0:k0 + kn], in0=iota[0:nq, 0:kn], scalar1=qrel[0:nq, gi:gi + 1],
                                                              scalar2=NEG, op0=ALU.is_gt, op1=ALU.mult), r=("qrel", "cst"), w=("sc%d" % gi,))
                        for h in range(8):
                            b = next_bank()
                            P.op("pe", lambda e: e.matmul(psb[b][0:nq, 0:kn], qi_t[32 * g:32 * g + 32, h, c0:c0 + nq], ki4[32 * g:32 * g + 32, j0:j0 + kn],
                                                          start=True, stop=True), r=("qi", "ki4"), w=("ps%d" % b,))
                            rb = rbufs[h % 2]
                            rk = "rb%d" % (h % 2)
                            P.op("act", lambda e: e.activation(out=rb[0:nq, 0:kn], in_=psb[b][0:nq, 0:kn], func=AF.Relu), r=("ps%d" % b,), w=(rk,))
                            P.op("dve", lambda e: e.scalar_tensor_tensor(out=scores[0:nq, k0:k0 + kn], in0=rb[0:nq, 0:kn], scalar=wtok[0:nq, qi_, h:h + 1],
                                                                         in1=scores[0:nq, k0:k0 + kn], op0=ALU.mult, op1=ALU.add),
                                 r=(rk, "wtok", "sc%d" % gi), w=("sc%d" % gi,))
                        yield

                def bisect(qi_):
                    c0, nq, KE = qbs[qi_]
                    ng = len(key_groups(KE))
                    sckeys = tuple("sc%d" % gi for gi in range(ng))
                    tb = 8 * (qi_ % 2)
                    tk = "t%d" % (qi_ % 2)
                    assert KE % 16 == 0
                    m = KE // 16
                    md = int(round(0.72 * m))
                    ma = m - md
                    jout = bass.AP(junks, 0, [[528, nq], [0, 16], [1, md]])
                    joutA = bass.AP(junkA, 0, [[528, nq], [0, 16], [1, ma]])
                    scd = scores[0:nq, 0:16 * md].rearrange("p (a b) -> p a b", a=16)
                    sca = scores[0:nq, 16 * md:KE].rearrange("p (a b) -> p a b", a=16)
                    PR, CA, CD, V, SG = (tiny[0:nq, tb + i:tb + i + 1] for i in (1, 2, 3, 4, 5))
                    P.op("dve", lambda e: e.reduce_max(out=tiny[0:nq, tb:tb + 1], in_=scores[0:nq, 0:KE], axis=AX.X), r=sckeys, w=(tk + "m",))
                    P.op("dve", lambda e: e.tensor_scalar(out=PR, in0=tiny[0:nq, tb:tb + 1], scalar1=-BIS_R / 2, scalar2=None, op0=ALU.add),
                         r=(tk + "m",), w=(tk + "p",))
                    vthr = 2.0 * TOPK - 16 * ma - 0.5
                    for it in range(BIS_IT):
                        P.op("act", lambda e: e.activation(out=joutA, in_=sca, func=AF.Sign, bias=PR, scale=-1.0, accum_out=CA),
                             r=sckeys + (tk + "p",), w=("junkA", tk + "ca"))
                        P.op("dve", lambda e: e.tensor_scalar(out=jout, in0=scd, scalar1=PR, scalar2=None, op0=ALU.is_ge, op1=ALU.add, accum_out=CD),
                             r=sckeys + (tk + "p",), w=("junk", tk + "cd"))
                        P.op("dve", lambda e: e.scalar_tensor_tensor(out=V, in0=CD, scalar=2.0, in1=CA, op0=ALU.mult, op1=ALU.subtract),
                             r=(tk + "cd", tk + "ca"), w=(tk + "v",))
                        if it < BIS_IT - 1:
                            c_next = BIS_R / 2 ** (it + 2)
                            P.op("dve", lambda e: e.tensor_scalar(out=SG, in0=V, scalar1=vthr, scalar2=2.0 * c_next, op0=ALU.is_ge, op1=ALU.mult),
                                 r=(tk + "v",), w=(tk + "s",))
                            P.op("dve", lambda e: e.scalar_tensor_tensor(out=PR, in0=SG, scalar=-c_next, in1=PR, op0=ALU.add, op1=ALU.add),
                                 r=(tk + "s", tk + "p"), w=(tk + "p",))
                        else:
                            c_last = BIS_R / 2 ** (it + 1)
                            P.op("dve", lambda e: e.tensor_scalar(out=SG, in0=V, scalar1=vthr, scalar2=-c_last, op0=ALU.is_lt, op1=ALU.mult),
                                 r=(tk + "v",), w=(tk + "s",))
                            P.op("dve", lambda e: e.tensor_tensor(out=tiny[0:nq, tb + 7:tb + 8], in0=SG, in1=PR, op=ALU.add),
                                 r=(tk + "s", tk + "p"), w=(tk + "t",))
                        yield

                def gen_mbias(qi_):
                    c0, nq, KE = qbs[qi_]
                    ng = len(key_groups(KE))
                    sckeys = tuple("sc%d" % gi for gi in range(ng))
                    tb = 8 * (qi_ % 2)
                    tk = "t%d" % (qi_ % 2)
                    P.op("dve", lambda e: e.tensor_scalar(out=mbias[0:nq, 0:KE], in0=scores[0:nq, 0:KE], scalar1=tiny[0:nq, tb + 7:tb + 8], scalar2=MASKNEG,
                                                          op0=ALU.is_lt, op1=ALU.mult), r=sckeys + (tk + "t",), w=("mbias",))

                def attention(qi_):
                    c0, nq, KE = qbs[qi_]
                    nkb = (KE + 127) // 128
                    I4v = I4[0:nq, :].rearrange("p (h t) -> p h t", h=4)[:, :, 0:nq]

                    def pv_ops(kb):
                        k0 = kb * 128
                        kn = min(128, KE - k0)
                        pb = pbufs[kb % 3]
                        cb_ = cbats[(kb // 4) % 2][:, kb % 4, :]
                        ck = "cbat%d" % ((kb // 4) % 2)
                        for hf in range(2):
                            pk = "pb%dh%d" % (kb % 3, hf)
                            P.op("pe", lambda e: e.matmul(psb[hf][:, 0:4 * nq].rearrange("p (h t) -> p h t", h=4), cb_[0:kn, :], pb[0:kn, 4 * hf:4 * hf + 4, 0:nq],
                                                          start=(kb == 0), stop=(kb == nkb - 1)), r=(pk, ck), w=("ps%d" % hf,))
                            P.op("pe", lambda e: e.matmul(psb[2 + hf][:, 0:4 * nq].rearrange("p (h t) -> p h t", h=4), ones_bf[0:kn, :], pb[0:kn, 4 * hf:4 * hf + 4, 0:nq],
                                                          start=(kb == 0), stop=(kb == nkb - 1)), r=(pk, "ones"), w=("ps%d" % (2 + hf),))

                    for kb in range(nkb):
                        k0 = kb * 128
                        kn = min(128, KE - k0)
                        if kb % 4 == 0:
                            par = (kb // 4) % 2
                            cbat = cbats[par]
                            ck = "cbat%d" % par
                            nb_ = min(4, nkb - kb)
                            for i_ in range(nb_):
                                kk0 = (kb + i_) * 128
                                kkn = min(128, KE - kk0)
                                P.op("pe", lambda e: e.transpose(pst[0:kkn, 512 * par + 128 * i_:512 * par + 128 * i_ + 128], cT[:, kk0:kk0 + kkn], ident_bf),
                                     r=("cT", "ident"), w=("pt",))
                            lastn = min(128, KE - (kb + nb_ - 1) * 128)
                            nfull = nb_ if lastn == 128 else nb_ - 1
                            if nfull > 0:
                                P.op("act", lambda e: e.copy(out=cbat[:, 0:nfull, :], in_=pst[:, 512 * par:512 * par + 128 * nfull].rearrange("p (a b) -> p a b", a=nfull)),
                                     r=("pt",), w=(ck,))
                            if nfull < nb_:
                                P.op("act", lambda e: e.copy(out=cbat[0:lastn, nfull, :], in_=pst[0:lastn, 512 * par + 128 * nfull:512 * par + 128 * nfull + 128]),
                                     r=("pt",), w=(ck,))
                        pb = pbufs[kb % 3]
                        for hf in range(2):
                            b = next_bank()
                            pk = "pb%dh%d" % (kb % 3, hf)
                            P.op("pe", lambda e: e.matmul(psb[b][0:kn, 0:4 * nq].rearrange("p (h t) -> p h t", h=4), cT[:, k0:k0 + kn],
                                                          q_t[:, 4 * hf:4 * hf + 4, c0:c0 + nq], start=True, stop=False), r=("cT", "q"), w=("ps%d" % b,))
                            P.op("pe", lambda e: e.matmul(psb[b][0:kn, 0:4 * nq].rearrange("p (h t) -> p h t", h=4), mbias[0:nq, k0:k0 + kn],
                                                          I4v, start=False, stop=True), r=("mbias", "I4"), w=("ps%d" % b,))
                            P.op("act", lambda e: e.activation(out=pb[0:kn, 4 * hf:4 * hf + 4, 0:nq],
                                                               in_=psb[b][0:kn, 0:4 * nq].rearrange("p (h t) -> p h t", h=4), func=AF.Exp, scale=ATTN_SCALE),
                                 r=("ps%d" % b,), w=(pk,))
                        if kb >= 1:
                            pv_ops(kb - 1)
                        yield
                    pv_ops(nkb - 1)
                    yield

                def normalize(qi_):
                    c0, nq, KE = qbs[qi_]
                    for hf in range(2):
                        P.op("dve", lambda e: e.reciprocal(out=rcb[:, 0:4 * nq], in_=psb[2 + hf][:, 0:4 * nq]), r=("ps%d" % (2 + hf),), w=("pb0h0", "pb0h1"))
                        P.op("dve", lambda e: e.tensor_tensor(out=attnT[:, 4 * hf:4 * hf + 4, c0:c0 + nq], in0=psb[hf][:, 0:4 * nq].rearrange("p (h t) -> p h t", h=4),
                                                              in1=rcb[:, 0:4 * nq].rearrange("p (h t) -> p h t", h=4), op=ALU.mult),
                             r=("ps%d" % hf, "pb0h0", "pb0h1"), w=("attnT",))

                def run(gen):
                    for _ in gen:
                        pass

                def side(qi_):
                    yield from idx_chain(qi_)
                    yield from bisect(qi_)

                nqb = len(qbs)
                q_first = 1 if skip_halo else 0
                run(side(q_first))
                gen_mbias(q_first)
                for qi_ in range(q_first, nqb):
                    if qi_ + 1 < nqb:
                        nside = len(key_groups(qbs[qi_ + 1][2])) + BIS_IT
                        S = side(qi_ + 1)
                        nA = (qbs[qi_][2] + 127) // 128 + 1
                        done = 0
                        for i_, _ in enumerate(attention(qi_)):
                            want = min(nside, ((i_ + 1) * nside) // (nA - 4 if nA > 8 else nA))
                            while done < want:
                                next(S, None)
                                done += 1
                        run(S)
                    else:
                        run(attention(qi_))
                    normalize(qi_)
                    if qi_ + 1 < nqb:
                        gen_mbias(qi_ + 1)

                P.barrier()
                if dbg:
                    for k in range(8):
                        P.dma("pool", "dbgp", lambda e, k=k: e.dma_start(out=dbg_attn[:, k, t0:t0 + n], in_=attnT[:, k, 0:n]), r=("attnT",), w=("dbgo",))
                if stop == "p2":
                    raise _Stop()
                bank_rr["pool"] = [0, 1, 2, 3, 4, 5, 6]
                xn = av_bf(0, (8, NMAX))
                m32 = av_f32(2176, (8, NMAX))
                mb = av_bf(6528, (8, NMAX))
                pa = av_bf(8704, (4, NMAX))
                pm = av_bf(9792, (4, NMAX))
                vc = av_bf(10880, (4, NMAX))
                ub = av_f32(11968, (16 + NMAX,))
                sA = av_f32(12528, (16 + NMAX,))
                sB = av_f32(13088, (16 + NMAX,))
                ccs = av_f32(13648, (NMAX,))
                vb = av_f32(14192, (2 + NMAX,))
                yb = av_f32(14738, (NMAX,))
                sg = av_f32(15282, (512,))
                tm = av_f32(15794, (512,))
                rms_norm_to(xn, t0, subs, smx["g1"], smkey, 12528)
                wv, wk = load_w(wsrc(w_in, U0, 512), 8, 512)
                pwv, pwk = load_w(wsrc(pool_w, 0, 128), 4, 128)
                for g in range(4):
                    wdw = 2 ** (g + 1)
                    P.op("dve", lambda e: e.memset(ub[:, 0:16], 0.0), w=("ub",))
                    for (s0, sn) in subs:
                        b = next_bank()
                        for k in range(8):
                            P.op("pe", lambda e, k=k: e.matmul(psb[b][:, 0:sn], wv[:, k, 128 * g:128 * g + 128], xn[:, k, s0:s0 + sn],
                                                                 start=(k == 0), stop=(k == 7)), r=("xn", wk), w=("ps%d" % b,))
                        P.op("act", lambda e: e.copy(out=ub[:, 16 + s0:16 + s0 + sn], in_=psb[b][:, 0:sn]), r=("ps%d" % b,), w=("ub",))
                    P.op("dve", lambda e: e.tensor_tensor(out=ub[:, 16:48], in0=ub[:, 16:48], in1=cmask, op=ALU.mult), r=("ub", "cst"), w=("ub",))
                    src, skey = ub, "ub"
                    tgl = [(sA, "sA"), (sB, "sB")]
                    W_ = 16 + n
                    sh = 1
                    for step in range(g + 1):
                        dst, dkey = tgl[step % 2]
                        P.op("dve", lambda e, src=src, dst=dst, sh=sh: e.tensor_tensor(out=dst[:, sh:W_], in0=src[:, sh:W_], in1=src[:, 0:W_ - sh], op=ALU.add),
                             r=(skey,), w=(dkey,))
                        if sh > 0:
                            P.op("dve", lambda e, src=src, dst=dst, sh=sh: e.tensor_copy(out=dst[:, 0:sh], in_=src[:, 0:sh]), r=(skey,), w=(dkey,))
                        src, skey = dst, dkey
                        sh *= 2
                    P.op("dve", lambda e, src=src: e.scalar_tensor_tensor(out=pa[:, g, 0:n], in0=src[:, 16:16 + n], scalar=1.0 / wdw, in1=ub[:, 16:16 + n],
                                                                          op0=ALU.mult, op1=ALU.subtract), r=(skey, "ub"), w=("pa",))
                    if ci == 0:
                        P.op("dve", lambda e, src=src: e.tensor_tensor(out=tm[:, 0:64], in0=src[:, 16:80], in1=invcnt[:, 64 * g:64 * g + 64], op=ALU.mult),
                             r=(skey, "cst"), w=("tm",))
                        P.op("dve", lambda e: e.tensor_tensor(out=pa[:, g, 0:64], in0=tm[:, 0:64], in1=ub[:, 16:80], op=ALU.subtract),
                             r=("tm", "ub"), w=("pa",))
                    for (s0, sn) in subs:
                        b = next_bank()
                        P.op("pe", lambda e: e.matmul(psb[b][:, 0:sn], pwv[:, g, :], pa[:, g, s0:s0 + sn], start=True, stop=True),
                             r=("pa", pwk), w=("ps%d" % b,))
                        P.op("dve", lambda e: e.tensor_scalar(out=pm[:, g, s0:s0 + sn], in0=psb[b][:, 0:sn], scalar1=smx["psc"][:, g:g + 1], scalar2=None,
                                                              op0=ALU.mult), r=("ps%d" % b, smkey), w=("pm",))

                def merge_pass(bi, ywt, ykc, ysrc, ykey, first, last):
                    for half in range(2):
                        gv, gk = load_w(wsrc(w_in, G0 + 1024 * bi + 512 * half, 512), 8, 512)
                        yv, yk = load_w(wsrc(ywt, 512 * half, 512), ykc, 512)
                        for jj in range(4):
                            j = 4 * half + jj
                            for (s0, sn) in subs:
                                bg_ = next_bank()
                                for k in range(8):
                                    P.op("pe", lambda e, k=k: e.matmul(psb[bg_][:, 0:sn], gv[:, k, 128 * jj:128 * jj + 128], xn[:, k, s0:s0 + sn],
                                                                         start=(k == 0), stop=(k == 7)), r=("xn", gk), w=("ps%d" % bg_,))
                                P.op("act", lambda e: e.activation(out=sg[:, 0:sn], in_=psb[bg_][:, 0:sn], func=AF.Sigmoid,
                                                                   bias=smx["bg"][:, 8 * bi + j:8 * bi + j + 1], scale=1.0),
                                     r=("ps%d" % bg_, smkey), w=("sg",))
                                by = next_bank()
                                for k in range(ykc):
                                    P.op("pe", lambda e, k=k: e.matmul(psb[by][:, 0:sn], yv[:, k, 128 * jj:128 * jj + 128], ysrc[:, k, s0:s0 + sn],
                                                                         start=(k == 0), stop=(k == ykc - 1)), r=(ykey, yk), w=("ps%d" % by,))
                                if first:
                                    P.op("dve", lambda e: e.tensor_tensor(out=m32[:, j, s0:s0 + sn], in0=psb[by][:, 0:sn], in1=sg[:, 0:sn], op=ALU.mult),
                                         r=("ps%d" % by, "sg"), w=("m32",))
                                else:
                                    P.op("dve", lambda e: e.tensor_tensor(out=tm[:, 0:sn], in0=psb[by][:, 0:sn], in1=sg[:, 0:sn], op=ALU.mult),
                                         r=("ps%d" % by, "sg"), w=("tm",))
                                    if last:
                                        P.op("dve", lambda e: e.tensor_tensor(out=mb[:, j, s0:s0 + sn], in0=tm[:, 0:sn], in1=m32[:, j, s0:s0 + sn], op=ALU.add),
                                             r=("tm", "m32"), w=("mb",))
                                    else:
                                        P.op("dve", lambda e: e.tensor_tensor(out=m32[:, j, s0:s0 + sn], in0=tm[:, 0:sn], in1=m32[:, j, s0:s0 + sn], op=ALU.add),
                                             r=("tm", "m32"), w=("m32",))

                merge_pass(0, w_a, 4, pm, "pm", True, False)

                wcc, kcc = load_w(wsrc(w_in, CC0, 512), 8, 512)
                wcu, kcu = load_w(wsrc(w_in, CU0, 512), 8, 512)
                wcb, kcb = load_w(wsrc(w_in, CB0, 512), 8, 512)
                for j in range(4):
                    for (s0, sn) in subs:
                        b = next_bank()
                        for k in range(8):
                            P.op("pe", lambda e, k=k: e.matmul(psb[b][:, 0:sn], wcc[:, k, 128 * j:128 * j + 128], xn[:, k, s0:s0 + sn],
                                                                 start=(k == 0), stop=(k == 7)), r=("xn", kcc), w=("ps%d" % b,))
                        P.op("act", lambda e: e.copy(out=ccs[:, s0:s0 + sn], in_=psb[b][:, 0:sn]), r=("ps%d" % b,), w=("ccs",))
                    P.op("dve", lambda e: e.memset(vb[:, 0:2], 0.0), w=("vb",))
                    for (s0, sn) in subs:
                        b = next_bank()
                        for k in range(8):
                            P.op("pe", lambda e, k=k: e.matmul(psb[b][:, 0:sn], wcu[:, k, 128 * j:128 * j + 128], xn[:, k, s0:s0 + sn],
                                                                 start=(k == 0), stop=(k == 7)), r=("xn", kcu), w=("ps%d" % b,))
                        P.op("dve", lambda e: e.tensor_tensor(out=vb[:, 2 + s0:2 + s0 + sn], in0=psb[b][:, 0:sn], in1=ccs[:, s0:s0 + sn], op=ALU.mult),
                             r=("ps%d" % b, "ccs"), w=("vb",))
                    P.op("dve", lambda e: e.tensor_tensor(out=vb[:, 2:34], in0=vb[:, 2:34], in1=cmask, op=ALU.mult), r=("vb", "cst"), w=("vb",))
                    cw = smx["cw"]
                    P.op("dve", lambda e: e.tensor_scalar(out=yb[:, 0:n], in0=vb[:, 2:2 + n], scalar1=cw[:, 3 * j + 2:3 * j + 3], scalar2=None, op0=ALU.mult),
                         r=("vb", smkey), w=("yb",))
                    P.op("dve", lambda e: e.scalar_tensor_tensor(out=yb[:, 0:n], in0=vb[:, 1:1 + n], scalar=cw[:, 3 * j + 1:3 * j + 2], in1=yb[:, 0:n],
                                                                 op0=ALU.mult, op1=ALU.add), r=("vb", "yb", smkey), w=("yb",))
                    P.op("dve", lambda e: e.scalar_tensor_tensor(out=yb[:, 0:n], in0=vb[:, 0:n], scalar=cw[:, 3 * j:3 * j + 1], in1=yb[:, 0:n],
                                                                 op0=ALU.mult, op1=ALU.add), r=("vb", "yb", smkey), w=("yb",))
                    for (s0, sn) in subs:
                        b = next_bank()
                        for k in range(8):
                            P.op("pe", lambda e, k=k: e.matmul(psb[b][:, 0:sn], wcb[:, k, 128 * j:128 * j + 128], xn[:, k, s0:s0 + sn],
                                                                 start=(k == 0), stop=(k == 7)), r=("xn", kcb), w=("ps%d" % b,))
                        P.op("dve", lambda e: e.tensor_tensor(out=vc[:, j, s0:s0 + sn], in0=psb[b][:, 0:sn], in1=yb[:, s0:s0 + sn], op=ALU.mult),
                             r=("ps%d" % b, "yb"), w=("vc",))
                merge_pass(2, w_c, 4, vc, "vc", False, False)
                merge_pass(1, w_b, 8, attnT, "attnT", False, True)

                for half in range(2):
                    wv, wk = load_w(wsrc(w_o, 512 * half, 512), 8, 512)
                    for jj in range(4):
                        j = 4 * half + jj
                        for (s0, sn) in subs:
                            b = next_bank()
                            for k in range(8):
                                P.op("pe", lambda e, k=k: e.matmul(psb[b][:, 0:sn], wv[:, k, 128 * jj:128 * jj + 128], mb[:, k, s0:s0 + sn],
                                                                     start=(k == 0), stop=(k == 7)), r=("mb", wk), w=("ps%d" % b,))
                            P.op("dve", lambda e: e.tensor_tensor(out=hT[:, j, t0 + s0:t0 + s0 + sn], in0=psb[b][:, 0:sn], in1=hT[:, j, t0 + s0:t0 + s0 + sn], op=ALU.add),
                                 r=("ps%d" % b, "h"), w=("h",))

                P.barrier()
                if dbg:
                    for k in range(8):
                        P.dma("pool", "dbgp", lambda e, k=k: e.dma_start(out=dbg_m[:, k, t0:t0 + n], in_=mb[:, k, 0:n]), r=("mb",), w=("dbgo",))
                        P.dma("sp", "out", lambda e, k=k: e.dma_start(out=dbg_hmid[:, k, t0:t0 + n], in_=hT[:, k, t0:t0 + n]), r=("h",), w=("dbgo",))
                    P.wait_all("act", ("out", "dbgp"))
                if stop == "p3":
                    raise _Stop()
                hn = av_bf(0, (8, NMAX))
                hid = av_bf(2176, (11, NMAX))
                sgf = av_f32(5168, (512,))
                rms_norm_to(hn, t0, subs, smx["g2"], smkey, 5680)
                for fh in range(2):
                    cbase = 1408 * fh
                    for (tc0, tcn) in ((0, 512), (512, 512), (1024, 384)):
                        gv, gk = load_w(wsrc(w_gu, cbase + tc0, tcn), 8, tcn)
                        uv, uk = load_w(wsrc(w_gu, DFF + cbase + tc0, tcn), 8, tcn)
                        for jj in range(tcn // 128):
                            jh = (tc0 // 128) + jj
                            for (s0, sn) in subs:
                                bg_ = next_bank()
                                for k in range(8):
                                    P.op("pe", lambda e, k=k: e.matmul(psb[bg_][:, 0:sn], gv[:, k, 128 * jj:128 * jj + 128], hn[:, k, s0:s0 + sn],
                                                                         start=(k == 0), stop=(k == 7)), r=("xn", gk), w=("ps%d" % bg_,))
                                P.op("act", lambda e: e.activation(out=sgf[:, 0:sn], in_=psb[bg_][:, 0:sn], func=AF.Silu), r=("ps%d" % bg_,), w=("sgf",))
                                bu = next_bank()
                                for k in range(8):
                                    P.op("pe", lambda e, k=k: e.matmul(psb[bu][:, 0:sn], uv[:, k, 128 * jj:128 * jj + 128], hn[:, k, s0:s0 + sn],
                                                                         start=(k == 0), stop=(k == 7)), r=("xn", uk), w=("ps%d" % bu,))
                                P.op("dve", lambda e: e.tensor_tensor(out=hid[:, jh, s0:s0 + sn], in0=psb[bu][:, 0:sn], in1=sgf[:, 0:sn], op=ALU.mult),
                                     r=("ps%d" % bu, "sgf"), w=("hid",))
                    for oc2 in range(4):
                        dv, dk = load_w(wsrc(w_dn, 256 * oc2, 256, r0=11 * fh, kc=11), 11, 256)
                        for o in range(2):
                            j = 2 * oc2 + o
                            for (s0, sn) in subs:
                                b = next_bank()
                                for k in range(11):
                                    P.op("pe", lambda e, k=k: e.matmul(psb[b][:, 0:sn], dv[:, k, 128 * o:128 * o + 128], hid[:, k, s0:s0 + sn],
                                                                         start=(k == 0), stop=(k == 10)), r=("hid", dk), w=("ps%d" % b,))
                                P.op("dve", lambda e: e.tensor_tensor(out=hT[:, j, t0 + s0:t0 + s0 + sn], in0=psb[b][:, 0:sn], in1=hT[:, j, t0 + s0:t0 + s0 + sn], op=ALU.add),
                                     r=("ps%d" % b, "h"), w=("h",))
                if stop == "c0":
                    raise _Stop()
                if post_chunk:
                    post_chunk(oi, ci)
            P.barrier()

        def cc_allgather(src_t, dst_t, rkeys, wkeys, tag):
            P.new_sem(tag, False)
            deps = P._collect(rkeys, wkeys)
            P._wait("pool", deps)
            ins = nc.gpsimd.collective_compute("AllGather", ALU.bypass, replica_groups=GROUPS4, ins=[src_t.ap().opt()], outs=[dst_t.ap().opt()])
            ins.then_inc(P.sems[tag])
            P.cnt[tag] = 1
            P._record(rkeys, wkeys, (tag, 1))

        stopped = False
        if fused:
            order = [("w_in", 1), ("w_in", 0), ("pool_w", 0), ("w_in", 3), ("w_a", 0), ("w_in", 2), ("w_in", 4), ("w_in", 5), ("w_c", 0),
                     ("w_in", 6), ("w_b", 0), ("w_o", 0)] + [("w_gu", i) for i in range(6)] + [("w_dn", i) for i in range(4)]
            assert len(order) == sum(len(wpieces(nm)) for nm in WSPEC)
            cvstate = {"sem": "cv0", "n": 0}

            def bounce(l, nm, pi):
                c0, w_ = wpieces(nm)[pi]
                P.dma("act", cvstate["sem"].replace("cv", "bn"), lambda e: e.dma_start(out=piece_in[(nm, l, pi)].ap(), in_=shard_in[(nm, l)][:, c0:c0 + w_]),
                      w=("pi_%s_%d_%d" % (nm, l, pi),))

            def cc_piece(l, nm, pi):
                cc_allgather(piece_in[(nm, l, pi)], piece_out[(nm, l, pi)], ("pi_%s_%d_%d" % (nm, l, pi),), ("pg_%s_%d_%d" % (nm, l, pi),),
                             "cc_%s_%d_%d" % (nm, l, pi))
                P.dma("pool", cvstate["sem"], lambda e: e.dma_start(out=piece_bf[(nm, l, pi)].ap(), in_=piece_out[(nm, l, pi)].ap()),
                      r=("pg_%s_%d_%d" % (nm, l, pi),), w=("pb_%s_%d_%d" % (nm, l, pi),))

            def cc_batch(sem, items):
                cvstate["sem"] = sem
                for (l_, nm_, pi_) in items:
                    bounce(l_, nm_, pi_)
                for (l_, nm_, pi_) in items:
                    cc_piece(l_, nm_, pi_)

            rest0 = [x for x in order if x not in (("w_in", 1), ("w_in", 0))]
            hooks0 = {(0, "p1"): lambda: cc_batch("cv2", [(0,) + x for x in rest0] + [(1,) + order[0]]),
                      (1, "p1"): lambda: cc_batch("cv3", [(1,) + x for x in order[1:8]]),
                      (2, "p1"): lambda: cc_batch("cv4", [(1,) + x for x in order[8:15]]),
                      (3, "p1"): lambda: cc_batch("cv5", [(1,) + x for x in order[15:22]])}
            cc_batch("cv0", [(0, "w_in", 1)])
            stage_A(WL[0]["w_in"], SM, "small")
            cc_allgather(kv_in, kv_all, ("kvout",), ("kvall",), "cckv0")
            cc_batch("cv1", [(0, "w_in", 0)])
            stage_B(WL[0], SM, "small", hooks0,
                    post_chunk=lambda oi, ci: stage_A(WL[1]["w_in"], SMA, "smallA", subs_a=[(CW * ci, 32), (CW * ci + 32, 512)], load=(oi == 0)))
            cc_allgather(kv_in, kv_all, ("kvout",), ("kvall",), "cckv1")
            stage_B(WL[1], SMA, "smallA", skip_halo=True)
        elif mode == "A0":
            stage_A(WL[0]["w_in"], SM, "small")
        else:
            try:
                stage_B(WL[lay], SM, "small")
            except _Stop:
                stopped = True
            if mode == "L0":
                if not stopped:
                    stage_A(WL[1]["w_in"], SMA, "smallA")
                for k in range(8):
                    P.dma("sp", "out", lambda e, k=k: e.dma_start(out=hout.rearrange("(k p) t -> p k t", p=128)[:, k, :], in_=hT[:, k, :]),
                          r=("h",), w=("hout",))
        if True:
            if do_final:
                P.barrier()
                bank_rr["pool"] = [0, 1, 2, 3, 4, 5, 6]
                sq = av_bf(0, (8, 512))
                rs = av_f32(2048, (512,))
                ob = av_f32(2560, (8, 512))
                outv = outT.rearrange("(k p) t -> p k t", p=128)
                for si in range(4):
                    t0 = CW * si + 32
                    n = 512
                    b = next_bank()
                    for k in range(8):
                        P.op("act", lambda e, k=k: e.activation(out=sq[:, k, 0:n], in_=hT[:, k, t0:t0 + n], func=AF.Square), r=("h",), w=("sq%d" % k,))
                    for k in range(8):
                        P.op("pe", lambda e, k=k: e.matmul(psb[b][:, 0:n], ones_bf[:], sq[:, k, 0:n], start=(k == 0), stop=(k == 7)),
                             r=("sq%d" % k, "ones"), w=("ps%d" % b,))
                    P.op("act", lambda e: e.activation(out=rs[:, 0:n], in_=psb[b][:, 0:n], func=AF.Sqrt, scale=1.0 / D, bias=EPS), r=("ps%d" % b,), w=("rs",))
                    P.op("dve", lambda e: e.reciprocal(out=rs[:, 0:n], in_=rs[:, 0:n]), r=("rs",), w=("rs",))
                    for k in range(8):
                        P.op("dve", lambda e, k=k: e.scalar_tensor_tensor(out=ob[:, k, 0:n], in0=hT[:, k, t0:t0 + n], scalar=SM["fg"][:, k:k + 1], in1=rs[:, 0:n],
                                                                           op0=ALU.mult, op1=ALU.mult), r=("h", "rs", "small"), w=("ob",))
                    for k in range(8):
                        P.dma("sp", "out", lambda e, k=k: e.dma_start(out=outv[:, k, 512 * si:512 * si + 512], in_=ob[:, k, 0:n]), r=("ob",), w=("outT",))
        P.wait_all("sp", ("out", "kv", "dbgp"))
        P.wait_all("sp", ("pe", "act", "dve"))
    nc_counts.update(P.cnt)
    return nc


def _core_positions(j):
    c = j % 4
    pos = np.zeros(NT, np.int64)
    for i in range(4):
        g = 4 * i + c
        pos[CW * i:CW * (i + 1)] = NMETA + 512 * g - HALO + np.arange(CW)
    return pos


def _const_pack(j):
    pos = _core_positions(j)
    cst = np.zeros((128, CSTW + NG), np.float32)
    cst[:, 0:512] = np.arange(512, dtype=np.float32)[None, :]
    cst[:, 512:640] = np.eye(128, dtype=np.float32)
    qp = np.zeros((128, 20), np.float32)
    for i in range(4):
        qp[0:32, 5 * i] = np.maximum(pos[CW * i:CW * i + 32], 0)
        for b_ in range(4):
            qp[:, 5 * i + 1 + b_] = np.maximum(pos[CW * i + 32 + 128 * b_:CW * i + 32 + 128 * (b_ + 1)], 0)
    cst[:, 640:660] = qp
    for i in range(4):
        cst[:, 660 + 32 * i:660 + 32 * (i + 1)] = (pos[CW * i:CW * i + 32] >= 0).astype(np.float32)[None, :]
    inv = np.zeros((4, 64), np.float32)
    for g in range(4):
        w = 2 ** (g + 1)
        p64 = pos[0:64]
        cnt = np.where(p64 >= 0, np.minimum(p64 + 1, w), w)
        inv[g] = 1.0 / cnt
    cst[:, 788:1044] = inv.reshape(1, 256)
    cst[:, CSTW:CSTW + NG] = -np.array([g[2] for g in ALL_GROUPS], np.float32)[None, :]
    return cst


def _small_pack(inp, l):
    sm = np.zeros((128, NSMALL), np.float32)
    sm[:, 0:8] = inp["norm1_g"][l].reshape(8, 128).T
    sm[:, 8:16] = inp["norm2_g"][l].reshape(8, 128).T
    sm[:, 16] = inp["kv_norm_g"][l]
    sm[:, 17:41] = inp["b_gate"][l].reshape(24, 128).T
    sm[:, 41:45] = inp["pool_scale"][l].reshape(4, 128).T
    cw = inp["conv_w"][l].reshape(3, 4, 128)
    sm[:, 45:57] = cw.transpose(2, 1, 0).reshape(128, 12)
    sm[:, 57:65] = inp["final_g"].reshape(8, 128).T
    return sm


def _layer_weights(inp, l):
    c = np.ascontiguousarray
    return dict(w_in=c(inp["w_in"][l]), pool_w=c(inp["pool_w"][l].reshape(512, 128)), w_a=c(inp["w_a"][l]), w_b=c(inp["w_b"][l]),
                w_c=c(inp["w_c"][l]), w_o=c(inp["w_o"][l]), w_gu=c(inp["w_gate_up"][l]), w_dn=c(inp["w_down"][l]))


def _gather_keys(kvs):
    out = []
    for b in range(2):
        parts = [kvs[4 * b][:, 16:NT]] + [kvs[4 * b + c][:, HALO:NT] for c in range(1, 4)]
        out.append(np.ascontiguousarray(np.concatenate(parts, axis=1)))
    return out


_NC_CACHE = {}


def _get_nc(mode):
    if mode not in _NC_CACHE:
        _NC_CACHE[mode] = build(mode)
    return _NC_CACHE[mode]


def kernel(**inp):
    inp = {k: np.asarray(v, dtype=np.float32) for k, v in inp.items()}
    x, meta = inp["x"], inp["meta"]
    cores = list(range(8))
    csts = [_const_pack(j) for j in cores]
    hins = []
    for j in cores:
        b = j // 4
        pos = _core_positions(j)
        hfull = np.concatenate([meta, x[b]], axis=0)
        xin = np.zeros((NT, D), np.float32)
        valid = pos >= 0
        xin[valid] = hfull[pos[valid]]
        hins.append(np.ascontiguousarray(xin.T))
    sm0, sm1 = _small_pack(inp, 0), _small_pack(inp, 1)
    WLs = [_layer_weights(inp, 0), _layer_weights(inp, 1)]
    in_maps = []
    for j in cores:
        r = j % 4
        m = dict(hin=hins[j], cst=csts[j], small=sm0, smallA=sm1, iota16=np.tile(np.arange(512, dtype=np.int16)[None, :], (128, 1)))
        for l in range(2):
            for name, (K, N, pc) in WSPEC.items():
                m["ws_%s_%d" % (name, l)] = np.ascontiguousarray(WLs[l][name][r * (K // 4):(r + 1) * (K // 4)])
        in_maps.append(m)
    res = run_bass_kernel_spmd(_get_nc("FUSED"), in_maps, core_ids=cores)
    out = np.zeros((2, SEQ, D), np.float32)
    for j in cores:
        b, c = j // 4, j % 4
        oT = res.results[j]["outT"]
        for i in range(4):
            g = 4 * i + c
            out[b, 512 * g:512 * (g + 1), :] = oT[:, 512 * i:512 * (i + 1)].T
    return out
```

```python
import numpy as np
from contextlib import ExitStack
import concourse.bass as bass
import concourse.mybir as mybir
from concourse.bass_utils import run_bass_kernel_spmd

F32 = mybir.dt.float32
BF16 = mybir.dt.bfloat16
U8 = mybir.dt.uint8
ALU = mybir.AluOpType
AF = mybir.ActivationFunctionType
AX = mybir.AxisListType

D = 1024
SEQ = 8192
NMETA = 16
LTOT = SEQ + NMETA
HALO = 32
OWN = 2048
CW = 544
NT = 4 * CW
KVW = 528
CSTW = 512 + 128 + 20 + 128 + 256
DEPTH = 2
EPS = 1e-6
N_IN = 6568
U0, Q0, C0, QI0, KI0, WI0, CB0, CC0, CU0, G0 = 0, 512, 1536, 1664, 1920, 1952, 1960, 2472, 2984, 3496
DFF = 2816
INDEX_SCALE = (8 ** -0.5) * (32 ** -0.5)
ATTN_SCALE = 128 ** -0.5
TOPK = 256
NEG = -2.0e4
MASKNEG = -30000.0
BIS_R = 16.0
BIS_IT = 16
TER_IT = 11
QUART = LTOT // 3
NGRP = 3
NSMALL = 72

CHUNKS = [(CW * i, CW, [(0, 32), (32, 512)]) for i in range(4)]
NMAX = 544
SUBS_ALL = [s_ for i in range(4) for s_ in ((CW * i, 32), (CW * i + 32, 512))]


def qblocks_of_chunk(ci):
    keh = NMETA + 512 * (4 * ci + 3)
    return [(0, 32, keh)] + [(32 + 128 * b_, 128, keh + 128 * (b_ + 1)) for b_ in range(4)]


def key_groups(KE):
    gs = []
    for g in range(NGRP):
        q0 = g * QUART
        q1 = min((g + 1) * QUART, KE)
        k = q0
        while k < q1:
            kn = min(512, q1 - k)
            gs.append((g, k - q0, k, kn))
            k += kn
    return gs


ALL_GROUPS = key_groups(LTOT)
NG = len(ALL_GROUPS)


nc_counts = {}

WSPEC = {
    "w_in": (1024, N_IN, 1024), "pool_w": (512, 128, 128), "w_a": (512, 1024, 1024), "w_b": (1024, 1024, 1024),
    "w_c": (512, 1024, 1024), "w_o": (1024, 1024, 1024), "w_gu": (1024, 2 * 2816, 1024), "w_dn": (2816, 1024, 256),
}
GROUPS4 = [[0, 1, 2, 3], [4, 5, 6, 7]]


def wpieces(name):
    K, N, pc = WSPEC[name]
    return [(c0, min(pc, N - c0)) for c0 in range(0, N, pc)]


class WMat:
    def __init__(self, K, pieces):
        self.K = K
        self.pieces = pieces

    def segs(self, col0, ncols, r0=0, kc=None):
        out = []
        kcc = self.K // 128 - r0 if kc is None else kc
        for (c0, pc, ap, key) in self.pieces:
            lo, hi = max(col0, c0), min(col0 + ncols, c0 + pc)
            if lo < hi:
                v = ap.rearrange("(kc p) n -> p kc n", p=128)
                out.append((lo - col0, hi - lo, v[:, r0:r0 + kcc, lo - c0:hi - c0], key))
        return out

    def seg1(self, col0, ncols):
        sg = self.segs(col0, ncols)
        assert len(sg) == 1
        return sg[0][2], sg[0][3]


class Prog:
    def __init__(self, nc, es):
        self.nc = nc
        self.es = es
        self.engs = {"pe": nc.tensor, "act": nc.scalar, "dve": nc.vector, "pool": nc.gpsimd, "sp": nc.sync}
        self.sems = {}
        self.cnt = {}
        self.isdma = {}
        self.waited = {}
        self.lastw = {}
        self.readers = {}
        for e in ("pe", "act", "dve", "pool"):
            self.new_sem(e, False)

    def new_sem(self, name, dma=True):
        self.sems[name] = self.es.enter_context(self.nc.semaphore("s_" + name))
        self.cnt[name] = 0
        self.isdma[name] = dma

    def _collect(self, r, w):
        deps = {}

        def add(tok):
            if tok is None:
                return
            s, v = tok
            if deps.get(s, 0) < v:
                deps[s] = v

        for k in r:
            add(self.lastw.get(k))
        for k in w:
            add(self.lastw.get(k))
            for s, v in self.readers.get(k, {}).items():
                add((s, v))
        return deps

    def _wait(self, e, deps):
        eng = self.engs[e]
        for s, v in deps.items():
            if self.isdma[s]:
                v = self.cnt[s]
            if self.waited.get((e, s), 0) < v:
                eng.wait_ge(self.sems[s], v)
                self.waited[(e, s)] = v

    def _record(self, r, w, tok):
        s, v = tok
        for k in r:
            d = self.readers.setdefault(k, {})
            if d.get(s, 0) < v:
                d[s] = v
        for k in w:
            self.lastw[k] = tok
            self.readers[k] = {}

    def op(self, e, fn, r=(), w=()):
        deps = self._collect(r, w)
        if e == "pe":
            deps.pop("pe", None)
        self._wait(e, deps)
        ins = fn(self.engs[e])
        self.cnt[e] += 1
        ins.then_inc(self.sems[e], 1)
        self._record(r, w, (e, self.cnt[e]))

    def dma(self, q, sem, fn, r=(), w=()):
        deps = self._collect(r, w)
        self._wait(q, deps)
        ins = fn(self.engs[q])
        self.cnt[sem] += 16
        ins.then_inc(self.sems[sem], 16)
        self._record(r, w, (sem, self.cnt[sem]))

    def barrier(self, engines=("pe", "act", "dve")):
        for e in engines:
            deps = {s: self.cnt[s] for s in ("pe", "act", "dve") if s != e and self.cnt[s] > 0}
            self._wait(e, deps)

    def wait_all(self, e, sems):
        self._wait(e, {s: self.cnt[s] for s in sems if self.cnt[s] > 0})


class _Stop(Exception):
    pass


def build(mode, dbg=False, stop=None):
    nc = bass.Bass("TRN2", target_bir_lowering=False)
    fused = mode == "FUSED"
    do_B = mode in ("L0", "L1")
    do_A = mode in ("A0", "L0")
    do_final = mode in ("L1", "FUSED")

    def din(name, shape):
        return nc.dram_tensor(name, shape, F32, kind="ExternalInput").ap()

    def dout(name, shape):
        return nc.dram_tensor(name, shape, F32, kind="ExternalOutput").ap()

    hin = din("hin", [D, NT])
    cst = din("cst", [128, CSTW + NG])
    small = din("small", [128, NSMALL])
    iota_in = nc.dram_tensor("iota16", [128, 512], mybir.dt.int16, kind="ExternalInput").ap()
    WL = [None, None]
    keys = None
    if fused:
        smallA = din("smallA", [128, NSMALL])
        shard_in, piece_in, piece_out, piece_bf = {}, {}, {}, {}
        for l in range(2):
            WL[l] = {}
            for name, (K, N, pc) in WSPEC.items():
                shard_in[(name, l)] = din("ws_%s_%d" % (name, l), [K // 4, N])
                pcs = []
                for pi, (c0, w_) in enumerate(wpieces(name)):
                    piece_in[(name, l, pi)] = nc.dram_tensor("pi_%s_%d_%d" % (name, l, pi), [K // 4, w_], F32)
                    piece_out[(name, l, pi)] = nc.dram_tensor("pg_%s_%d_%d" % (name, l, pi), [K, w_], F32)
                    piece_bf[(name, l, pi)] = nc.dram_tensor("pb_%s_%d_%d" % (name, l, pi), [K, w_], BF16)
                    pcs.append((c0, w_, piece_bf[(name, l, pi)].ap(), "pb_%s_%d_%d" % (name, l, pi)))
                WL[l][name] = WMat(K, pcs)
        kv_in = nc.dram_tensor("kv_in", [160, 4 * KVW], BF16)
        kv_all = nc.dram_tensor("kv_all", [640, 4 * KVW], BF16)
    else:
        lay = {"A0": 0, "L0": 0, "L1": 1}[mode]
        names = ["w_in"] if mode == "A0" else list(WSPEC.keys())
        WL[lay] = {nm: WMat(WSPEC[nm][0], [(0, WSPEC[nm][1], din(nm, [WSPEC[nm][0], WSPEC[nm][1]]), "ext")]) for nm in names}
        if mode == "L0":
            smallA = din("smallA", [128, NSMALL])
            WL[1] = {"w_in": WMat(1024, [(0, N_IN, din("w_inA", [D, N_IN]), "ext")])}
        elif mode == "A0":
            smallA = small
        if do_B:
            keys = din("keys", [160, LTOT])
        if do_A:
            kvout = dout("kvout", [160, NT])
    if mode == "L0":
        hout = dout("hout", [D, NT])
    if do_final:
        outT = dout("outT", [D, OWN])
    if dbg:
        dbg_attn = dout("dbg_attn", [D, NT]).rearrange("(k p) t -> p k t", p=128)
        dbg_m = dout("dbg_m", [D, NT]).rearrange("(k p) t -> p k t", p=128)
        dbg_hmid = dout("dbg_hmid", [D, NT]).rearrange("(k p) t -> p k t", p=128)

    with ExitStack() as es:
        P = Prog(nc, es)
        for s in ("ld", "ldp", "kv", "keys", "out", "w0", "w1", "w2", "ws", "dbgp", "bn0", "bn1", "bn2", "bn3", "bn4", "bn5", "cv0", "cv1", "cv2", "cv3", "cv4", "cv5"):
            P.new_sem(s)

        def sb(name, shape, dt):
            return es.enter_context(nc.sbuf_tensor(name, shape, dt))

        hT = sb("hT", [128, 8, NT], F32)
        cT = sb("cT", [128, LTOT], BF16)
        ki4 = sb("ki4", [128, QUART], BF16)
        attnT = sb("attnT", [128, 8, NMAX], BF16)
        cstt = sb("cstt", [128, CSTW - 640 + NG], F32)
        iota = sb("iota_sb", [128, 512], mybir.dt.int16)
        smt = sb("smt", [128, NSMALL], F32)
        smtA = sb("smtA", [128, NSMALL], F32)
        ones_bf = sb("ones_bf", [128, 128], BF16)
        I4 = sb("I4", [128, 512], BF16)
        ident_bf = I4[:, 0:128]
        wsm = sb("wsm", [128, 8, 168], BF16)
        wslots = [sb("wslot%d" % i, [128, 4096], BF16) for i in range(3)]
        tiny = sb("tiny", [128, 16], F32)
        junks = sb("junks", [128, 528], U8)
        junkA = sb("junkA", [128, 528], U8)
        wtok = sb("wtok", [128, 5, 8], F32)
        qrel = sb("qrel", [128, NG], F32)
        ARENA_W = 19736
        arena = sb("arena", [128, ARENA_W], F32)

        qpos = cstt[:, 0:20]
        cmask_all = cstt[:, 20:148]
        invcnt = cstt[:, 148:404]
        k0neg = cstt[:, CSTW - 640:CSTW - 640 + NG]

        psb = [es.enter_context(nc.psum_tensor("psb%d" % i, [128, 512], F32)) for i in range(7)]
        pst = es.enter_context(nc.psum_tensor("pst", [128, 1024], BF16))
        bank_rr = {"pool": [4, 5, 6], "i": 0}

        def next_bank():
            pool = bank_rr["pool"]
            b = pool[bank_rr["i"] % len(pool)]
            bank_rr["i"] += 1
            return b

        tr_rr = {"i": 0}

        def next_tr():
            i = tr_rr["i"] % 8
            tr_rr["i"] += 1
            return i

        def av_f32(off, shape):
            n = int(np.prod(shape))
            ap = arena[:, off:off + n]
            if len(shape) == 2:
                return ap.rearrange("p (a b) -> p a b", a=shape[0])
            return ap

        def av_bf(off, shape):
            n = int(np.prod(shape))
            assert n % 2 == 0
            ap = arena[:, off:off + n // 2].bitcast(BF16)
            if len(shape) == 2:
                return ap.rearrange("p (a b) -> p a b", a=shape[0])
            return ap

        wstate = {"i": 0}
        wq = "sp" if fused else "pool"

        def load_w(segs, kc, ncols):
            assert kc * ncols <= 4096
            i = wstate["i"] % 3
            wstate["i"] += 1
            view = wslots[i][:, 0:kc * ncols].rearrange("p (k n) -> p k n", k=kc)
            key = "wslot%d" % i
            for (off, n_, ap_, pkey) in segs:
                P.dma(wq, "w%d" % i, lambda e, off=off, n_=n_, ap_=ap_: e.dma_start(out=view[:, :, off:off + n_], in_=ap_), r=(pkey,), w=(key,))
            return view, key

        def wsrc(W, col0, ncols, r0=0, kc=None):
            return W.segs(col0, ncols, r0, kc)

        P.dma("sp", "ld", lambda e: e.dma_start(out=cstt[:], in_=cst[:, 640:CSTW + NG]), w=("cst",))
        P.dma("sp", "ld", lambda e: e.dma_start(out=iota[:], in_=iota_in), w=("cst",))
        for i4 in range(4):
            P.dma("pool", "ldp", lambda e, i4=i4: e.dma_start(out=I4[:, 128 * i4:128 * i4 + 128], in_=cst[:, 512:640]), w=("I4", "ident"))
        P.dma("sp", "ld", lambda e: e.dma_start(out=smt[:], in_=small), w=("small",))
        if do_A or fused:
            P.dma("sp", "ld", lambda e: e.dma_start(out=smtA[:], in_=smallA), w=("smallA",))
        hin_v = hin.rearrange("(k p) t -> p k t", p=128)
        for k in range(8):
            P.dma("sp", "ld", lambda e, k=k: e.dma_start(out=hT[:, k, :], in_=hin_v[:, k, :]), w=("h",))
        P.op("dve", lambda e: e.memset(ones_bf[:], 1.0), w=("ones",))

        def smv(t):
            return dict(g1=t[:, 0:8], g2=t[:, 8:16], kvg=t[:, 16:17], bg=t[:, 17:41], psc=t[:, 41:45],
                        cw=t[:, 45:57], fg=t[:, 57:65])

        SM = smv(smt)
        SMA = smv(smtA)

        def rms_norm_to(xn_view, t0, subs, gvec, gkey, sq_off):
            sq = av_bf(sq_off, (8, 512))
            rs = av_f32(sq_off + 2048, (512,))
            for (s0, n) in subs:
                b = next_bank()
                for k in range(8):
                    P.op("act", lambda e, k=k: e.activation(out=sq[:, k, 0:n], in_=hT[:, k, t0 + s0:t0 + s0 + n], func=AF.Square),
                         r=("h",), w=("sq%d" % k,))
                for k in range(8):
                    P.op("pe", lambda e, k=k: e.matmul(psb[b][:, 0:n], ones_bf[:], sq[:, k, 0:n], start=(k == 0), stop=(k == 7)),
                         r=("sq%d" % k, "ones"), w=("ps%d" % b,))
                P.op("act", lambda e: e.activation(out=rs[:, 0:n], in_=psb[b][:, 0:n], func=AF.Sqrt, scale=1.0 / D, bias=EPS),
                     r=("ps%d" % b,), w=("rs",))
                P.op("dve", lambda e: e.reciprocal(out=rs[:, 0:n], in_=rs[:, 0:n]), r=("rs",), w=("rs",))
                for k in range(8):
                    P.op("dve", lambda e, k=k: e.scalar_tensor_tensor(out=xn_view[:, k, s0:s0 + n], in0=hT[:, k, t0 + s0:t0 + s0 + n],
                                                                       scalar=gvec[:, k:k + 1], in1=rs[:, 0:n], op0=ALU.mult, op1=ALU.mult),
                         r=("h", "rs", gkey), w=("xn",))

        def stage_A(w_in_l, smx, smkey, subs_a=None, load=True):
            P.barrier()
            bank_rr["pool"] = [0, 1, 2, 3, 4, 5, 6]
            if subs_a is None:
                subs_a = SUBS_ALL
            if load:
                ap_c, k_c = w_in_l.seg1(C0, 128)
                ap_k, k_k = w_in_l.seg1(KI0, 32)
                P.dma(wq, "ws", lambda e: e.dma_start(out=wsm[:, :, 0:128], in_=ap_c), r=(k_c,), w=("wsm",))
                P.dma(wq, "ws", lambda e: e.dma_start(out=wsm[:, :, 128:160], in_=ap_k), r=(k_k,), w=("wsm",))
            xn = av_bf(0, (8, NMAX))
            c32 = av_f32(8192, (512,))
            csq = av_bf(8192 + 512, (512,))
            rs2 = av_f32(8192 + 1024, (512,))
            ki32 = av_f32(8192 + 1536, (512,))
            cbf = av_bf(8192 + 2048, (512,))
            kibf = av_bf(8192 + 2304, (512,))
            for (t0, n) in subs_a:
                rms_norm_to(xn, t0, [(0, n)], smx["g1"], smkey, 4352)
                b = next_bank()
                for k in range(8):
                    P.op("pe", lambda e, k=k: e.matmul(psb[b][:, 0:n], wsm[:, k, 0:128], xn[:, k, 0:n], start=(k == 0), stop=(k == 7)),
                         r=("xn", "wsm"), w=("ps%d" % b,))
                P.op("act", lambda e: e.activation(out=csq[:, 0:n], in_=psb[b][:, 0:n], func=AF.Square), r=("ps%d" % b,), w=("csq",))
                b2 = next_bank()
                P.op("pe", lambda e: e.matmul(psb[b2][:, 0:n], ones_bf[:], csq[:, 0:n], start=True, stop=True),
                     r=("csq", "ones"), w=("ps%d" % b2,))
                P.op("act", lambda e: e.activation(out=rs2[:, 0:n], in_=psb[b2][:, 0:n], func=AF.Sqrt, scale=1.0 / 128, bias=EPS),
                     r=("ps%d" % b2,), w=("rs2",))
                P.op("dve", lambda e: e.reciprocal(out=rs2[:, 0:n], in_=rs2[:, 0:n]), r=("rs2",), w=("rs2",))
                cdst = cbf if fused else c32
                kdst = kibf if fused else ki32
                P.op("dve", lambda e: e.scalar_tensor_tensor(out=cdst[:, 0:n], in0=psb[b][:, 0:n], scalar=smx["kvg"], in1=rs2[:, 0:n],
                                                             op0=ALU.mult, op1=ALU.mult), r=("ps%d" % b, "rs2", smkey), w=("c32",))
                b3 = next_bank()
                for k in range(8):
                    P.op("pe", lambda e, k=k: e.matmul(psb[b3][0:32, 0:n], wsm[:, k, 128:160], xn[:, k, 0:n], start=(k == 0), stop=(k == 7)),
                         r=("xn", "wsm"), w=("ps%d" % b3,))
                P.op("act", lambda e: e.copy(out=kdst[0:32, 0:n], in_=psb[b3][0:32, 0:n]), r=("ps%d" % b3,), w=("ki32",))
                if fused:
                    ich, cc_ = t0 // CW, t0 % CW
                    lo = max(cc_, 16)
                    d0, d1 = KVW * ich + lo - 16, KVW * ich + cc_ + n - 16
                    P.dma("sp", "kv", lambda e: e.dma_start(out=kv_in.ap()[0:128, d0:d1], in_=cdst[:, lo - cc_:n]), r=("c32",), w=("kvout",))
                    P.dma("sp", "kv", lambda e: e.dma_start(out=kv_in.ap()[128:160, d0:d1], in_=kdst[0:32, lo - cc_:n]), r=("ki32",), w=("kvout",))
                else:
                    P.dma("sp", "kv", lambda e: e.dma_start(out=kvout[0:128, t0:t0 + n], in_=c32[:, 0:n]), r=("c32",), w=("kvout",))
                    P.dma("sp", "kv", lambda e: e.dma_start(out=kvout[128:160, t0:t0 + n], in_=ki32[0:32, 0:n]), r=("ki32",), w=("kvout",))
            P.barrier()

        def stage_B(Wd, smx, smkey, hooks=None, skip_halo=False, post_chunk=None):
            w_in, pool_w, w_a, w_b, w_c, w_o, w_gu, w_dn = (Wd[k_] for k_ in ("w_in", "pool_w", "w_a", "w_b", "w_c", "w_o", "w_gu", "w_dn"))
            def load_keys():
              if fused:
                kva = kv_all.ap()

                def key_src(k0, k1):
                    segs = [(0, NMETA, 0, 0)] + [(NMETA + 512 * g_, 512, g_ % 4, KVW * (g_ // 4) + 16) for g_ in range(16)]
                    out = []
                    for (g0, gn, r, col) in segs:
                        lo, hi = max(k0, g0), min(k1, g0 + gn)
                        if lo < hi:
                            out.append((lo - k0, hi - lo, r, col + lo - g0))
                    return out

                for (off, n_, r, c0_) in key_src(0, LTOT):
                    P.dma("sp", "keys", lambda e, off=off, n_=n_, r=r, c0_=c0_: e.dma_start(out=cT[:, off:off + n_], in_=kva[160 * r:160 * r + 128, c0_:c0_ + n_]),
                          r=("kvall",), w=("cT",))
                for g in range(NGRP):
                    for (off, n_, r, c0_) in key_src(g * QUART, (g + 1) * QUART):
                        P.dma("sp", "keys", lambda e, off=off, n_=n_, r=r, c0_=c0_, g=g: e.dma_start(
                            out=ki4[32 * g:32 * g + 32, off:off + n_], in_=kva[160 * r + 128:160 * r + 160, c0_:c0_ + n_]), r=("kvall",), w=("ki4",))
              else:
                P.dma("pool", "keys", lambda e: e.dma_start(out=cT[:], in_=keys[0:128, :]), w=("cT",))
                for g in range(NGRP):
                    P.dma("pool", "keys", lambda e, g=g: e.dma_start(out=ki4[32 * g:32 * g + 32, :], in_=keys[128:160, g * QUART:(g + 1) * QUART]),
                          w=("ki4",))

            ap_w, k_w = w_in.seg1(WI0, 8)
            P.dma(wq, "ws", lambda e: e.dma_start(out=wsm[:, :, 160:168], in_=ap_w), r=(k_w,), w=("wsm",))

            for oi, ci in enumerate((3, 2, 1, 0)):
                t0, n, subs = CHUNKS[ci]
                qbs = qblocks_of_chunk(ci)
                cmask = cmask_all[:, 32 * ci:32 * ci + 32]
                P.barrier()
                bank_rr["pool"] = [0, 1, 2, 3, 4, 5, 6]
                xnA = av_bf(0, (8, NMAX))
                scores = av_f32(0, (LTOT,))
                q_t = av_bf(8208, (8, NMAX))
                qi_t = av_bf(10384, (8, NMAX))
                mbias = av_bf(12560, (LTOT,))
                rbufs = [av_f32(16664 + 512 * i, (512,)) for i in range(2)]
                pbufs = [av_bf(17688 + 512 * i, (8, 128)) for i in range(3)]
                cbats = [av_bf(19224 + 256 * i, (4, 128)) for i in range(2)]
                rcb = av_f32(17688, (512,))
                rms_norm_to(xnA, t0, subs, smx["g1"], smkey, 12560)
                for half in range(2):
                    wv, wk = load_w(wsrc(w_in, Q0 + 512 * half, 512), 8, 512)
                    for hh in range(4):
                        h = half * 4 + hh
                        for (s0, sn) in subs:
                            b = next_bank()
                            for k in range(8):
                                P.op("pe", lambda e, k=k: e.matmul(psb[b][:, 0:sn], wv[:, k, 128 * hh:128 * hh + 128], xnA[:, k, s0:s0 + sn],
                                                                     start=(k == 0), stop=(k == 7)), r=("xn", wk), w=("ps%d" % b,))
                            P.op("act", lambda e: e.copy(out=q_t[:, h, s0:s0 + sn], in_=psb[b][:, 0:sn]), r=("ps%d" % b,), w=("q",))
                for half in range(2):
                    i = wstate["i"] % 3
                    wstate["i"] += 1
                    key = "wslot%d" % i
                    view = wslots[i][:, 0:3072].rearrange("p (k h r d) -> p k h r d", k=8, h=4, r=3)
                    for hh in range(4):
                        h = half * 4 + hh
                        ap_q, k_q = w_in.seg1(QI0 + 32 * h, 32)
                        for r_ in range(3):
                            P.dma(wq, "w%d" % i, lambda e, hh=hh, r_=r_, ap_q=ap_q: e.dma_start(
                                out=view[:, :, hh, r_, :], in_=ap_q), r=(k_q,), w=(key,))
                    for hh in range(4):
                        h = half * 4 + hh
                        lw = wslots[i][:, 0:3072].rearrange("p (k h m) -> p k h m", k=8, h=4)
                        for (s0, sn) in subs:
                            b = next_bank()
                            for k in range(8):
                                P.op("pe", lambda e, k=k: e.matmul(psb[b][0:96, 0:sn], lw[:, k, hh, :], xnA[:, k, s0:s0 + sn],
                                                                     start=(k == 0), stop=(k == 7)), r=("xn", key), w=("ps%d" % b,))
                            P.op("act", lambda e: e.copy(out=qi_t[0:96, h, s0:s0 + sn], in_=psb[b][0:96, 0:sn]), r=("ps%d" % b,), w=("qi",))
                for qi_, (c0, nq, KE) in enumerate(qbs):
                    b = next_bank()
                    for k in range(8):
                        P.op("pe", lambda e, k=k: e.matmul(psb[b][0:nq, 0:8], xnA[:, k, c0:c0 + nq], wsm[:, k, 160:168],
                                                             start=(k == 0), stop=(k == 7)), r=("xn", "wsm"), w=("ps%d" % b,))
                    P.op("dve", lambda e: e.tensor_scalar(out=wtok[0:nq, qi_, :], in0=psb[b][0:nq, 0:8], scalar1=INDEX_SCALE, scalar2=None,
                                                          op0=ALU.mult), r=("ps%d" % b,), w=("wtok",))

                if oi == 0:
                    load_keys()
                if hooks and (oi, "p1") in hooks:
                    hooks[(oi, "p1")]()
                if stop == "p1":
                    raise _Stop()
                P.barrier()
                bank_rr["pool"] = [4, 5, 6]
                def idx_chain(qi_):
                    c0, nq, KE = qbs[qi_]
                    groups = key_groups(KE)
                    ng = len(groups)
                    qcol = 5 * ci + qi_
                    P.op("dve", lambda e: e.tensor_scalar(out=qrel[0:nq, 0:ng], in0=k0neg[0:nq, 0:ng], scalar1=qpos[0:nq, qcol:qcol + 1],
                                                          scalar2=None, op0=ALU.add), r=("cst",), w=("qrel",))
                    for gi, (g, j0, k0, kn) in enumerate(groups):
                        P.op("dve", lambda e: e.tensor_scalar(out=scores[0:nq, k0:k0 + kn], in0=iota[0:nq, 0:kn], scalar1=qrel[0:nq, gi:gi + 1],
                                                              scalar2=NEG, op0=ALU.is_gt, op1=ALU.mult), r=("qrel", "cst"), w=("sc%d" % gi,))
                        for h in range(8):
                            b = next_bank()
                            P.op("pe", lambda e: e.matmul(psb[b][0:nq, 0:kn], qi_t[32 * g:32 * g + 32, h, c0:c0 + nq], ki4[32 * g:32 * g + 32, j0:j0 + kn],
                                                          start=True, stop=True), r=("qi", "ki4"), w=("ps%d" % b,))
                            rb = rbufs[h % 2]
                            rk = "rb%d" % (h % 2)
                            P.op("act", lambda e: e.activation(out=rb[0:nq, 0:kn], in_=psb[b][0:nq, 0:kn], func=AF.Relu), r=("ps%d" % b,), w=(rk,))
                            P.op("dve", lambda e: e.scalar_tensor_tensor(out=scores[0:nq, k0:k0 + kn], in0=rb[0:nq, 0:kn], scalar=wtok[0:nq, qi_, h:h + 1],
                                                                         in1=scores[0:nq, k0:k0 + kn], op0=ALU.mult, op1=ALU.add),
                                 r=(rk, "wtok", "sc%d" % gi), w=("sc%d" % gi,))
                        yield

                def bisect(qi_):
                    c0, nq, KE = qbs[qi_]
                    ng = len(key_groups(KE))
                    sckeys = tuple("sc%d" % gi for gi in range(ng))
                    tb = 8 * (qi_ % 2)
                    tk = "t%d" % (qi_ % 2)
                    assert KE % 16 == 0
                    m = KE // 16
                    md = int(round(0.55 * m))
                    ma = m - md
                    jout = bass.AP(junks, 0, [[528, nq], [0, 16], [1, md]])
                    joutA = bass.AP(junkA, 0, [[528, nq], [0, 16], [1, ma]])
                    scd = scores[0:nq, 0:16 * md].rearrange("p (a b) -> p a b", a=16)
                    sca = scores[0:nq, 16 * md:KE].rearrange("p (a b) -> p a b", a=16)
                    PR, CA, CD, V, SG = (tiny[0:nq, tb + i:tb + i + 1] for i in (1, 2, 3, 4, 5))
                    P.op("dve", lambda e: e.reduce_max(out=tiny[0:nq, tb:tb + 1], in_=scores[0:nq, 0:KE], axis=AX.X), r=sckeys, w=(tk + "m",))
                    P.op("dve", lambda e: e.tensor_scalar(out=PR, in0=tiny[0:nq, tb:tb + 1], scalar1=-BIS_R / 2, scalar2=None, op0=ALU.add),
                         r=(tk + "m",), w=(tk + "p",))
                    vthr = 2.0 * TOPK - 16 * ma - 0.5
                    for it in range(BIS_IT):
                        P.op("act", lambda e: e.activation(out=joutA, in_=sca, func=AF.Sign, bias=PR, scale=-1.0, accum_out=CA),
                             r=sckeys + (tk + "p",), w=("junkA", tk + "ca"))
                        P.op("dve", lambda e: e.tensor_scalar(out=jout, in0=scd, scalar1=PR, scalar2=None, op0=ALU.is_ge, op1=ALU.add, accum_out=CD),
                             r=sckeys + (tk + "p",), w=("junk", tk + "cd"))
                        P.op("dve", lambda e: e.scalar_tensor_tensor(out=V, in0=CD, scalar=2.0, in1=CA, op0=ALU.mult, op1=ALU.subtract),
                             r=(tk + "cd", tk + "ca"), w=(tk + "v",))
                        if it < BIS_IT - 1:
                            c_next = BIS_R / 2 ** (it + 2)
                            P.op("dve", lambda e: e.tensor_scalar(out=SG, in0=V, scalar1=vthr, scalar2=2.0 * c_next, op0=ALU.is_ge, op1=ALU.mult),
                                 r=(tk + "v",), w=(tk + "s",))
                            P.op("dve", lambda e: e.scalar_tensor_tensor(out=PR, in0=SG, scalar=-c_next, in1=PR, op0=ALU.add, op1=ALU.add),
                                 r=(tk + "s", tk + "p"), w=(tk + "p",))
                        else:
                            c_last = BIS_R / 2 ** (it + 1)
                            P.op("dve", lambda e: e.tensor_scalar(out=SG, in0=V, scalar1=vthr, scalar2=-c_last, op0=ALU.is_lt, op1=ALU.mult),
                                 r=(tk + "v",), w=(tk + "s",))
                            P.op("dve", lambda e: e.tensor_tensor(out=tiny[0:nq, tb + 7:tb + 8], in0=SG, in1=PR, op=ALU.add),
                                 r=(tk + "s", tk + "p"), w=(tk + "t",))
                        yield

                def gen_mbias(qi_):
                    c0, nq, KE = qbs[qi_]
                    ng = len(key_groups(KE))
                    sckeys = tuple("sc%d" % gi for gi in range(ng))
                    tb = 8 * (qi_ % 2)
                    tk = "t%d" % (qi_ % 2)
                    P.op("dve", lambda e: e.tensor_scalar(out=mbias[0:nq, 0:KE], in0=scores[0:nq, 0:KE], scalar1=tiny[0:nq, tb + 7:tb + 8], scalar2=MASKNEG,
                                                          op0=ALU.is_lt, op1=ALU.mult), r=sckeys + (tk + "t",), w=("mbias",))

                def attention(qi_):
                    c0, nq, KE = qbs[qi_]
                    nkb = (KE + 127) // 128
                    I4v = I4[0:nq, :].rearrange("p (h t) -> p h t", h=4)[:, :, 0:nq]

                    def pv_ops(kb):
                        k0 = kb * 128
                        kn = min(128, KE - k0)
                        pb = pbufs[kb % 3]
                        cb_ = cbats[(kb // 4) % 2][:, kb % 4, :]
                        ck = "cbat%d" % ((kb // 4) % 2)
                        for hf in range(2):
                            pk = "pb%dh%d" % (kb % 3, hf)
                            P.op("pe", lambda e: e.matmul(psb[hf][:, 0:4 * nq].rearrange("p (h t) -> p h t", h=4), cb_[0:kn, :], pb[0:kn, 4 * hf:4 * hf + 4, 0:nq],
                                                          start=(kb == 0), stop=(kb == nkb - 1)), r=(pk, ck), w=("ps%d" % hf,))
                            P.op("pe", lambda e: e.matmul(psb[2 + hf][:, 0:4 * nq].rearrange("p (h t) -> p h t", h=4), ones_bf[0:kn, :], pb[0:kn, 4 * hf:4 * hf + 4, 0:nq],
                                                          start=(kb == 0), stop=(kb == nkb - 1)), r=(pk, "ones"), w=("ps%d" % (2 + hf),))

                    for kb in range(nkb):
                        k0 = kb * 128
                        kn = min(128, KE - k0)
                        if kb % 4 == 0:
                            par = (kb // 4) % 2
                            cbat = cbats[par]
                            ck = "cbat%d" % par
                            nb_ = min(4, nkb - kb)
                            for i_ in range(nb_):
                                kk0 = (kb + i_) * 128
                                kkn = min(128, KE - kk0)
                                P.op("pe", lambda e: e.transpose(pst[0:kkn, 512 * par + 128 * i_:512 * par + 128 * i_ + 128], cT[:, kk0:kk0 + kkn], ident_bf),
                                     r=("cT", "ident"), w=("pt",))
                            lastn = min(128, KE - (kb + nb_ - 1) * 128)
                            nfull = nb_ if lastn == 128 else nb_ - 1
                            if nfull > 0:
                                P.op("act", lambda e: e.copy(out=cbat[:, 0:nfull, :], in_=pst[:, 512 * par:512 * par + 128 * nfull].rearrange("p (a b) -> p a b", a=nfull)),
                                     r=("pt",), w=(ck,))
                            if nfull < nb_:
                                P.op("act", lambda e: e.copy(out=cbat[0:lastn, nfull, :], in_=pst[0:lastn, 512 * par + 128 * nfull:512 * par + 128 * nfull + 128]),
                                     r=("pt",), w=(ck,))
                        pb = pbufs[kb % 3]
                        for hf in range(2):
                            b = next_bank()
                            pk = "pb%dh%d" % (kb % 3, hf)
                            P.op("pe", lambda e: e.matmul(psb[b][0:kn, 0:4 * nq].rearrange("p (h t) -> p h t", h=4), cT[:, k0:k0 + kn],
                                                          q_t[:, 4 * hf:4 * hf + 4, c0:c0 + nq], start=True, stop=False), r=("cT", "q"), w=("ps%d" % b,))
                            P.op("pe", lambda e: e.matmul(psb[b][0:kn, 0:4 * nq].rearrange("p (h t) -> p h t", h=4), mbias[0:nq, k0:k0 + kn],
                                                          I4v, start=False, stop=True), r=("mbias", "I4"), w=("ps%d" % b,))
                            P.op("act", lambda e: e.activation(out=pb[0:kn, 4 * hf:4 * hf + 4, 0:nq],
                                                               in_=psb[b][0:kn, 0:4 * nq].rearrange("p (h t) -> p h t", h=4), func=AF.Exp, scale=ATTN_SCALE),
                                 r=("ps%d" % b,), w=(pk,))
                        if kb >= 1:
                            pv_ops(kb - 1)
                        yield
                    pv_ops(nkb - 1)
                    yield

                def normalize(qi_):
                    c0, nq, KE = qbs[qi_]
                    for hf in range(2):
                        P.op("dve", lambda e: e.reciprocal(out=rcb[:, 0:4 * nq], in_=psb[2 + hf][:, 0:4 * nq]), r=("ps%d" % (2 + hf),), w=("pb0h0", "pb0h1"))
                        P.op("dve", lambda e: e.tensor_tensor(out=attnT[:, 4 * hf:4 * hf + 4, c0:c0 + nq], in0=psb[hf][:, 0:4 * nq].rearrange("p (h t) -> p h t", h=4),
                                                              in1=rcb[:, 0:4 * nq].rearrange("p (h t) -> p h t", h=4), op=ALU.mult),
                             r=("ps%d" % hf, "pb0h0", "pb0h1"), w=("attnT",))

                def run(gen):
                    for _ in gen:
                        pass

                def side(qi_):
                    yield from idx_chain(qi_)
                    yield from bisect(qi_)

                nqb = len(qbs)
                q_first = 1 if skip_halo else 0
                run(side(q_first))
                gen_mbias(q_first)
                for qi_ in range(q_first, nqb):
                    if qi_ + 1 < nqb:
                        nside = len(key_groups(qbs[qi_ + 1][2])) + BIS_IT
                        S = side(qi_ + 1)
                        nA = (qbs[qi_][2] + 127) // 128 + 1
                        done = 0
                        for i_, _ in enumerate(attention(qi_)):
                            want = min(nside, ((i_ + 1) * nside) // (nA - 4 if nA > 8 else nA))
                            while done < want:
                                next(S, None)
                                done += 1
                        run(S)
                    else:
                        run(attention(qi_))
                    normalize(qi_)
                    if qi_ + 1 < nqb:
                        gen_mbias(qi_ + 1)

                P.barrier()
                if dbg:
                    for k in range(8):
                        P.dma("pool", "dbgp", lambda e, k=k: e.dma_start(out=dbg_attn[:, k, t0:t0 + n], in_=attnT[:, k, 0:n]), r=("attnT",), w=("dbgo",))
                if stop == "p2":
                    raise _Stop()
                bank_rr["pool"] = [0, 1, 2, 3, 4, 5, 6]
                xn = av_bf(0, (8, NMAX))
                m32 = av_f32(2176, (8, NMAX))
                mb = av_bf(6528, (8, NMAX))
                pa = av_bf(8704, (4, NMAX))
                pm = av_bf(9792, (4, NMAX))
                vc = av_bf(10880, (4, NMAX))
                ub = av_f32(11968, (16 + NMAX,))
                sA = av_f32(12528, (16 + NMAX,))
                sB = av_f32(13088, (16 + NMAX,))
                ccs = av_f32(13648, (NMAX,))
                vb = av_f32(14192, (2 + NMAX,))
                yb = av_f32(14738, (NMAX,))
                sg = av_f32(15282, (512,))
                tm = av_f32(15794, (512,))
                rms_norm_to(xn, t0, subs, smx["g1"], smkey, 12528)
                wv, wk = load_w(wsrc(w_in, U0, 512), 8, 512)
                pwv, pwk = load_w(wsrc(pool_w, 0, 128), 4, 128)
                for g in range(4):
                    wdw = 2 ** (g + 1)
                    P.op("dve", lambda e: e.memset(ub[:, 0:16], 0.0), w=("ub",))
                    for (s0, sn) in subs:
                        b = next_bank()
                        for k in range(8):
                            P.op("pe", lambda e, k=k: e.matmul(psb[b][:, 0:sn], wv[:, k, 128 * g:128 * g + 128], xn[:, k, s0:s0 + sn],
                                                                 start=(k == 0), stop=(k == 7)), r=("xn", wk), w=("ps%d" % b,))
                        P.op("act", lambda e: e.copy(out=ub[:, 16 + s0:16 + s0 + sn], in_=psb[b][:, 0:sn]), r=("ps%d" % b,), w=("ub",))
                    P.op("dve", lambda e: e.tensor_tensor(out=ub[:, 16:48], in0=ub[:, 16:48], in1=cmask, op=ALU.mult), r=("ub", "cst"), w=("ub",))
                    src, skey = ub, "ub"
                    tgl = [(sA, "sA"), (sB, "sB")]
                    W_ = 16 + n
                    sh = 1
                    for step in range(g + 1):
                        dst, dkey = tgl[step % 2]
                        P.op("dve", lambda e, src=src, dst=dst, sh=sh: e.tensor_tensor(out=dst[:, sh:W_], in0=src[:, sh:W_], in1=src[:, 0:W_ - sh], op=ALU.add),
                             r=(skey,), w=(dkey,))
                        if sh > 0:
                            P.op("dve", lambda e, src=src, dst=dst, sh=sh: e.tensor_copy(out=dst[:, 0:sh], in_=src[:, 0:sh]), r=(skey,), w=(dkey,))
                        src, skey = dst, dkey
                        sh *= 2
                    P.op("dve", lambda e, src=src: e.scalar_tensor_tensor(out=pa[:, g, 0:n], in0=src[:, 16:16 + n], scalar=1.0 / wdw, in1=ub[:, 16:16 + n],
                                                                          op0=ALU.mult, op1=ALU.subtract), r=(skey, "ub"), w=("pa",))
                    if ci == 0:
                        P.op("dve", lambda e, src=src: e.tensor_tensor(out=tm[:, 0:64], in0=src[:, 16:80], in1=invcnt[:, 64 * g:64 * g + 64], op=ALU.mult),
                             r=(skey, "cst"), w=("tm",))
                        P.op("dve", lambda e: e.tensor_tensor(out=pa[:, g, 0:64], in0=tm[:, 0:64], in1=ub[:, 16:80], op=ALU.subtract),
                             r=("tm", "ub"), w=("pa",))
                    for (s0, sn) in subs:
                        b = next_bank()
                        P.op("pe", lambda e: e.matmul(psb[b][:, 0:sn], pwv[:, g, :], pa[:, g, s0:s0 + sn], start=True, stop=True),
                             r=("pa", pwk), w=("ps%d" % b,))
                        P.op("dve", lambda e: e.tensor_scalar(out=pm[:, g, s0:s0 + sn], in0=psb[b][:, 0:sn], scalar1=smx["psc"][:, g:g + 1], scalar2=None,
                                                              op0=ALU.mult), r=("ps%d" % b, smkey), w=("pm",))

                def merge_pass(bi, ywt, ykc, ysrc, ykey, first, last):
                    for half in range(2):
                        gv, gk = load_w(wsrc(w_in, G0 + 1024 * bi + 512 * half, 512), 8, 512)
                        yv, yk = load_w(wsrc(ywt, 512 * half, 512), ykc, 512)
                        for jj in range(4):
                            j = 4 * half + jj
                            for (s0, sn) in subs:
                                bg_ = next_bank()
                                for k in range(8):
                                    P.op("pe", lambda e, k=k: e.matmul(psb[bg_][:, 0:sn], gv[:, k, 128 * jj:128 * jj + 128], xn[:, k, s0:s0 + sn],
                                                                         start=(k == 0), stop=(k == 7)), r=("xn", gk), w=("ps%d" % bg_,))
                                P.op("act", lambda e: e.activation(out=sg[:, 0:sn], in_=psb[bg_][:, 0:sn], func=AF.Sigmoid,
                                                                   bias=smx["bg"][:, 8 * bi + j:8 * bi + j + 1], scale=1.0),
                                     r=("ps%d" % bg_, smkey), w=("sg",))
                                by = next_bank()
                                for k in range(ykc):
                                    P.op("pe", lambda e, k=k: e.matmul(psb[by][:, 0:sn], yv[:, k, 128 * jj:128 * jj + 128], ysrc[:, k, s0:s0 + sn],
                                                                         start=(k == 0), stop=(k == ykc - 1)), r=(ykey, yk), w=("ps%d" % by,))
                                if first:
                                    P.op("dve", lambda e: e.tensor_tensor(out=m32[:, j, s0:s0 + sn], in0=psb[by][:, 0:sn], in1=sg[:, 0:sn], op=ALU.mult),
                                         r=("ps%d" % by, "sg"), w=("m32",))
                                else:
                                    P.op("dve", lambda e: e.tensor_tensor(out=tm[:, 0:sn], in0=psb[by][:, 0:sn], in1=sg[:, 0:sn], op=ALU.mult),
                                         r=("ps%d" % by, "sg"), w=("tm",))
                                    if last:
                                        P.op("dve", lambda e: e.tensor_tensor(out=mb[:, j, s0:s0 + sn], in0=tm[:, 0:sn], in1=m32[:, j, s0:s0 + sn], op=ALU.add),
                                             r=("tm", "m32"), w=("mb",))
                                    else:
                                        P.op("dve", lambda e: e.tensor_tensor(out=m32[:, j, s0:s0 + sn], in0=tm[:, 0:sn], in1=m32[:, j, s0:s0 + sn], op=ALU.add),
                                             r=("tm", "m32"), w=("m32",))

                merge_pass(0, w_a, 4, pm, "pm", True, False)

                wcc, kcc = load_w(wsrc(w_in, CC0, 512), 8, 512)
                wcu, kcu = load_w(wsrc(w_in, CU0, 512), 8, 512)
                wcb, kcb = load_w(wsrc(w_in, CB0, 512), 8, 512)
                for j in range(4):
                    for (s0, sn) in subs:
                        b = next_bank()
                        for k in range(8):
                            P.op("pe", lambda e, k=k: e.matmul(psb[b][:, 0:sn], wcc[:, k, 128 * j:128 * j + 128], xn[:, k, s0:s0 + sn],
                                                                 start=(k == 0), stop=(k == 7)), r=("xn", kcc), w=("ps%d" % b,))
                        P.op("act", lambda e: e.copy(out=ccs[:, s0:s0 + sn], in_=psb[b][:, 0:sn]), r=("ps%d" % b,), w=("ccs",))
                    P.op("dve", lambda e: e.memset(vb[:, 0:2], 0.0), w=("vb",))
                    for (s0, sn) in subs:
                        b = next_bank()
                        for k in range(8):
                            P.op("pe", lambda e, k=k: e.matmul(psb[b][:, 0:sn], wcu[:, k, 128 * j:128 * j + 128], xn[:, k, s0:s0 + sn],
                                                                 start=(k == 0), stop=(k == 7)), r=("xn", kcu), w=("ps%d" % b,))
                        P.op("dve", lambda e: e.tensor_tensor(out=vb[:, 2 + s0:2 + s0 + sn], in0=psb[b][:, 0:sn], in1=ccs[:, s0:s0 + sn], op=ALU.mult),
                             r=("ps%d" % b, "ccs"), w=("vb",))
                    P.op("dve", lambda e: e.tensor_tensor(out=vb[:, 2:34], in0=vb[:, 2:34], in1=cmask, op=ALU.mult), r=("vb", "cst"), w=("vb",))
                    cw = smx["cw"]
                    P.op("dve", lambda e: e.tensor_scalar(out=yb[:, 0:n], in0=vb[:, 2:2 + n], scalar1=cw[:, 3 * j + 2:3 * j + 3], scalar2=None, op0=ALU.mult),
                         r=("vb", smkey), w=("yb",))
                    P.op("dve", lambda e: e.scalar_tensor_tensor(out=yb[:, 0:n], in0=vb[:, 1:1 + n], scalar=cw[:, 3 * j + 1:3 * j + 2], in1=yb[:, 0:n],
                                                                 op0=ALU.mult, op1=ALU.add), r=("vb", "yb", smkey), w=("yb",))
                    P.op("dve", lambda e: e.scalar_tensor_tensor(out=yb[:, 0:n], in0=vb[:, 0:n], scalar=cw[:, 3 * j:3 * j + 1], in1=yb[:, 0:n],
                                                                 op0=ALU.mult, op1=ALU.add), r=("vb", "yb", smkey), w=("yb",))
                    for (s0, sn) in subs:
                        b = next_bank()
                        for k in range(8):
                            P.op("pe", lambda e, k=k: e.matmul(psb[b][:, 0:sn], wcb[:, k, 128 * j:128 * j + 128], xn[:, k, s0:s0 + sn],
                                                                 start=(k == 0), stop=(k == 7)), r=("xn", kcb), w=("ps%d" % b,))
                        P.op("dve", lambda e: e.tensor_tensor(out=vc[:, j, s0:s0 + sn], in0=psb[b][:, 0:sn], in1=yb[:, s0:s0 + sn], op=ALU.mult),
                             r=("ps%d" % b, "yb"), w=("vc",))
                merge_pass(2, w_c, 4, vc, "vc", False, False)
                merge_pass(1, w_b, 8, attnT, "attnT", False, True)

                for half in range(2):
                    wv, wk = load_w(wsrc(w_o, 512 * half, 512), 8, 512)
                    for jj in range(4):
                        j = 4 * half + jj
                        for (s0, sn) in subs:
                            b = next_bank()
                            for k in range(8):
                                P.op("pe", lambda e, k=k: e.matmul(psb[b][:, 0:sn], wv[:, k, 128 * jj:128 * jj + 128], mb[:, k, s0:s0 + sn],
                                                                     start=(k == 0), stop=(k == 7)), r=("mb", wk), w=("ps%d" % b,))
                            P.op("dve", lambda e: e.tensor_tensor(out=hT[:, j, t0 + s0:t0 + s0 + sn], in0=psb[b][:, 0:sn], in1=hT[:, j, t0 + s0:t0 + s0 + sn], op=ALU.add),
                                 r=("ps%d" % b, "h"), w=("h",))

                P.barrier()
                if dbg:
                    for k in range(8):
                        P.dma("pool", "dbgp", lambda e, k=k: e.dma_start(out=dbg_m[:, k, t0:t0 + n], in_=mb[:, k, 0:n]), r=("mb",), w=("dbgo",))
                        P.dma("sp", "out", lambda e, k=k: e.dma_start(out=dbg_hmid[:, k, t0:t0 + n], in_=hT[:, k, t0:t0 + n]), r=("h",), w=("dbgo",))
                    P.wait_all("act", ("out", "dbgp"))
                if stop == "p3":
                    raise _Stop()
                hn = av_bf(0, (8, NMAX))
                hid = av_bf(2176, (11, NMAX))
                sgf = av_f32(5168, (512,))
                rms_norm_to(hn, t0, subs, smx["g2"], smkey, 5680)
                for fh in range(2):
                    cbase = 1408 * fh
                    for (tc0, tcn) in ((0, 512), (512, 512), (1024, 384)):
                        gv, gk = load_w(wsrc(w_gu, cbase + tc0, tcn), 8, tcn)
                        uv, uk = load_w(wsrc(w_gu, DFF + cbase + tc0, tcn), 8, tcn)
                        for jj in range(tcn // 128):
                            jh = (tc0 // 128) + jj
                            for (s0, sn) in subs:
                                bg_ = next_bank()
                                for k in range(8):
                                    P.op("pe", lambda e, k=k: e.matmul(psb[bg_][:, 0:sn], gv[:, k, 128 * jj:128 * jj + 128], hn[:, k, s0:s0 + sn],
                                                                         start=(k == 0), stop=(k == 7)), r=("xn", gk), w=("ps%d" % bg_,))
                                P.op("act", lambda e: e.activation(out=sgf[:, 0:sn], in_=psb[bg_][:, 0:sn], func=AF.Silu), r=("ps%d" % bg_,), w=("sgf",))
                                bu = next_bank()
                                for k in range(8):
                                    P.op("pe", lambda e, k=k: e.matmul(psb[bu][:, 0:sn], uv[:, k, 128 * jj:128 * jj + 128], hn[:, k, s0:s0 + sn],
                                                                         start=(k == 0), stop=(k == 7)), r=("xn", uk), w=("ps%d" % bu,))
                                P.op("dve", lambda e: e.tensor_tensor(out=hid[:, jh, s0:s0 + sn], in0=psb[bu][:, 0:sn], in1=sgf[:, 0:sn], op=ALU.mult),
                                     r=("ps%d" % bu, "sgf"), w=("hid",))
                    for oc2 in range(4):
                        dv, dk = load_w(wsrc(w_dn, 256 * oc2, 256, r0=11 * fh, kc=11), 11, 256)
                        for o in range(2):
                            j = 2 * oc2 + o
                            for (s0, sn) in subs:
                                b = next_bank()
                                for k in range(11):
                                    P.op("pe", lambda e, k=k: e.matmul(psb[b][:, 0:sn], dv[:, k, 128 * o:128 * o + 128], hid[:, k, s0:s0 + sn],
                                                                         start=(k == 0), stop=(k == 10)), r=("hid", dk), w=("ps%d" % b,))
                                P.op("dve", lambda e: e.tensor_tensor(out=hT[:, j, t0 + s0:t0 + s0 + sn], in0=psb[b][:, 0:sn], in1=hT[:, j, t0 + s0:t0 + s0 + sn], op=ALU.add),
                                     r=("ps%d" % b, "h"), w=("h",))
                if stop == "c0":
                    raise _Stop()
                if post_chunk:
                    post_chunk(oi, ci)
            P.barrier()

        def cc_allgather(src_t, dst_t, rkeys, wkeys, tag):
            P.new_sem(tag, False)
            deps = P._collect(rkeys, wkeys)
            P._wait("pool", deps)
            ins = nc.gpsimd.collective_compute("AllGather", ALU.bypass, replica_groups=GROUPS4, ins=[src_t.ap().opt()], outs=[dst_t.ap().opt()])
            ins.then_inc(P.sems[tag])
            P.cnt[tag] = 1
            P._record(rkeys, wkeys, (tag, 1))

        stopped = False
        if fused:
            order = [("w_in", 1), ("w_in", 0), ("pool_w", 0), ("w_in", 3), ("w_a", 0), ("w_in", 2), ("w_in", 4), ("w_in", 5), ("w_c", 0),
                     ("w_in", 6), ("w_b", 0), ("w_o", 0)] + [("w_gu", i) for i in range(6)] + [("w_dn", i) for i in range(4)]
            assert len(order) == sum(len(wpieces(nm)) for nm in WSPEC)
            cvstate = {"sem": "cv0", "n": 0}

            def bounce(l, nm, pi):
                c0, w_ = wpieces(nm)[pi]
                P.dma("act", cvstate["sem"].replace("cv", "bn"), lambda e: e.dma_start(out=piece_in[(nm, l, pi)].ap(), in_=shard_in[(nm, l)][:, c0:c0 + w_]),
                      w=("pi_%s_%d_%d" % (nm, l, pi),))

            def cc_piece(l, nm, pi):
                cc_allgather(piece_in[(nm, l, pi)], piece_out[(nm, l, pi)], ("pi_%s_%d_%d" % (nm, l, pi),), ("pg_%s_%d_%d" % (nm, l, pi),),
                             "cc_%s_%d_%d" % (nm, l, pi))
                P.dma("pool", cvstate["sem"], lambda e: e.dma_start(out=piece_bf[(nm, l, pi)].ap(), in_=piece_out[(nm, l, pi)].ap()),
                      r=("pg_%s_%d_%d" % (nm, l, pi),), w=("pb_%s_%d_%d" % (nm, l, pi),))

            def cc_batch(sem, items):
                cvstate["sem"] = sem
                for (l_, nm_, pi_) in items:
                    bounce(l_, nm_, pi_)
                for (l_, nm_, pi_) in items:
                    cc_piece(l_, nm_, pi_)

            rest0 = [x for x in order if x not in (("w_in", 1), ("w_in", 0))]
            hooks0 = {(0, "p1"): lambda: cc_batch("cv2", [(0,) + x for x in rest0] + [(1,) + order[0]]),
                      (1, "p1"): lambda: cc_batch("cv3", [(1,) + x for x in order[1:8]]),
                      (2, "p1"): lambda: cc_batch("cv4", [(1,) + x for x in order[8:15]]),
                      (3, "p1"): lambda: cc_batch("cv5", [(1,) + x for x in order[15:22]])}
            cc_batch("cv0", [(0, "w_in", 1)])
            stage_A(WL[0]["w_in"], SM, "small")
            cc_allgather(kv_in, kv_all, ("kvout",), ("kvall",), "cckv0")
            cc_batch("cv1", [(0, "w_in", 0)])
            stage_B(WL[0], SM, "small", hooks0,
                    post_chunk=lambda oi, ci: stage_A(WL[1]["w_in"], SMA, "smallA", subs_a=[(CW * ci, 32), (CW * ci + 32, 512)], load=(oi == 0)))
            cc_allgather(kv_in, kv_all, ("kvout",), ("kvall",), "cckv1")
            stage_B(WL[1], SMA, "smallA", skip_halo=True)
        elif mode == "A0":
            stage_A(WL[0]["w_in"], SM, "small")
        else:
            try:
                stage_B(WL[lay], SM, "small")
            except _Stop:
                stopped = True
            if mode == "L0":
                if not stopped:
                    stage_A(WL[1]["w_in"], SMA, "smallA")
                for k in range(8):
                    P.dma("sp", "out", lambda e, k=k: e.dma_start(out=hout.rearrange("(k p) t -> p k t", p=128)[:, k, :], in_=hT[:, k, :]),
                          r=("h",), w=("hout",))
        if True:
            if do_final:
                P.barrier()
                bank_rr["pool"] = [0, 1, 2, 3, 4, 5, 6]
                sq = av_bf(0, (8, 512))
                rs = av_f32(2048, (512,))
                ob = av_f32(2560, (8, 512))
                outv = outT.rearrange("(k p) t -> p k t", p=128)
                for si in range(4):
                    t0 = CW * si + 32
                    n = 512
                    b = next_bank()
                    for k in range(8):
                        P.op("act", lambda e, k=k: e.activation(out=sq[:, k, 0:n], in_=hT[:, k, t0:t0 + n], func=AF.Square), r=("h",), w=("sq%d" % k,))
                    for k in range(8):
                        P.op("pe", lambda e, k=k: e.matmul(psb[b][:, 0:n], ones_bf[:], sq[:, k, 0:n], start=(k == 0), stop=(k == 7)),
                             r=("sq%d" % k, "ones"), w=("ps%d" % b,))
                    P.op("act", lambda e: e.activation(out=rs[:, 0:n], in_=psb[b][:, 0:n], func=AF.Sqrt, scale=1.0 / D, bias=EPS), r=("ps%d" % b,), w=("rs",))
                    P.op("dve", lambda e: e.reciprocal(out=rs[:, 0:n], in_=rs[:, 0:n]), r=("rs",), w=("rs",))
                    for k in range(8):
                        P.op("dve", lambda e, k=k: e.scalar_tensor_tensor(out=ob[:, k, 0:n], in0=hT[:, k, t0:t0 + n], scalar=SM["fg"][:, k:k + 1], in1=rs[:, 0:n],
                                                                           op0=ALU.mult, op1=ALU.mult), r=("h", "rs", "small"), w=("ob",))
                    for k in range(8):
                        P.dma("sp", "out", lambda e, k=k: e.dma_start(out=outv[:, k, 512 * si:512 * si + 512], in_=ob[:, k, 0:n]), r=("ob",), w=("outT",))
        P.wait_all("sp", ("out", "kv", "dbgp"))
        P.wait_all("sp", ("pe", "act", "dve"))
    nc_counts.update(P.cnt)
    return nc


def _core_positions(j):
    c = j % 4
    pos = np.zeros(NT, np.int64)
    for i in range(4):
        g = 4 * i + c
        pos[CW * i:CW * (i + 1)] = NMETA + 512 * g - HALO + np.arange(CW)
    return pos


def _const_pack(j):
    pos = _core_positions(j)
    cst = np.zeros((128, CSTW + NG), np.float32)
    cst[:, 0:512] = np.arange(512, dtype=np.float32)[None, :]
    cst[:, 512:640] = np.eye(128, dtype=np.float32)
    qp = np.zeros((128, 20), np.float32)
    for i in range(4):
        qp[0:32, 5 * i] = np.maximum(pos[CW * i:CW * i + 32], 0)
        for b_ in range(4):
            qp[:, 5 * i + 1 + b_] = np.maximum(pos[CW * i + 32 + 128 * b_:CW * i + 32 + 128 * (b_ + 1)], 0)
    cst[:, 640:660] = qp
    for i in range(4):
        cst[:, 660 + 32 * i:660 + 32 * (i + 1)] = (pos[CW * i:CW * i + 32] >= 0).astype(np.float32)[None, :]
    inv = np.zeros((4, 64), np.float32)
    for g in range(4):
        w = 2 ** (g + 1)
        p64 = pos[0:64]
        cnt = np.where(p64 >= 0, np.minimum(p64 + 1, w), w)
        inv[g] = 1.0 / cnt
    cst[:, 788:1044] = inv.reshape(1, 256)
    cst[:, CSTW:CSTW + NG] = -np.array([g[2] for g in ALL_GROUPS], np.float32)[None, :]
    return cst


def _small_pack(inp, l):
    sm = np.zeros((128, NSMALL), np.float32)
    sm[:, 0:8] = inp["norm1_g"][l].reshape(8, 128).T
    sm[:, 8:16] = inp["norm2_g"][l].reshape(8, 128).T
    sm[:, 16] = inp["kv_norm_g"][l]
    sm[:, 17:41] = inp["b_gate"][l].reshape(24, 128).T
    sm[:, 41:45] = inp["pool_scale"][l].reshape(4, 128).T
    cw = inp["conv_w"][l].reshape(3, 4, 128)
    sm[:, 45:57] = cw.transpose(2, 1, 0).reshape(128, 12)
    sm[:, 57:65] = inp["final_g"].reshape(8, 128).T
    return sm


def _layer_weights(inp, l):
    c = np.ascontiguousarray
    return dict(w_in=c(inp["w_in"][l]), pool_w=c(inp["pool_w"][l].reshape(512, 128)), w_a=c(inp["w_a"][l]), w_b=c(inp["w_b"][l]),
                w_c=c(inp["w_c"][l]), w_o=c(inp["w_o"][l]), w_gu=c(inp["w_gate_up"][l]), w_dn=c(inp["w_down"][l]))


def _gather_keys(kvs):
    out = []
    for b in range(2):
        parts = [kvs[4 * b][:, 16:NT]] + [kvs[4 * b + c][:, HALO:NT] for c in range(1, 4)]
        out.append(np.ascontiguousarray(np.concatenate(parts, axis=1)))
    return out


_NC_CACHE = {}


def _get_nc(mode):
    if mode not in _NC_CACHE:
        _NC_CACHE[mode] = build(mode)
    return _NC_CACHE[mode]


def kernel(**inp):
    inp = {k: np.asarray(v, dtype=np.float32) for k, v in inp.items()}
    x, meta = inp["x"], inp["meta"]
    cores = list(range(8))
    csts = [_const_pack(j) for j in cores]
    hins = []
    for j in cores:
        b = j // 4
        pos = _core_positions(j)
        hfull = np.concatenate([meta, x[b]], axis=0)
        xin = np.zeros((NT, D), np.float32)
        valid = pos >= 0
        xin[valid] = hfull[pos[valid]]
        hins.append(np.ascontiguousarray(xin.T))
    sm0, sm1 = _small_pack(inp, 0), _small_pack(inp, 1)
    WLs = [_layer_weights(inp, 0), _layer_weights(inp, 1)]
    in_maps = []
    for j in cores:
        r = j % 4
        m = dict(hin=hins[j], cst=csts[j], small=sm0, smallA=sm1, iota16=np.tile(np.arange(512, dtype=np.int16)[None, :], (128, 1)))
        for l in range(2):
            for name, (K, N, pc) in WSPEC.items():
                m["ws_%s_%d" % (name, l)] = np.ascontiguousarray(WLs[l][name][r * (K // 4):(r + 1) * (K // 4)])
        in_maps.append(m)
    res = run_bass_kernel_spmd(_get_nc("FUSED"), in_maps, core_ids=cores)
    out = np.zeros((2, SEQ, D), np.float32)
    for j in cores:
        b, c = j // 4, j % 4
        oT = res.results[j]["outT"]
        for i in range(4):
            g = 4 * i + c
            out[b, 512 * g:512 * (g + 1), :] = oT[:, 512 * i:512 * (i + 1)].T
    return out
```

```python
import numpy as np
from contextlib import ExitStack
import concourse.bass as bass
import concourse.mybir as mybir
from concourse.bass_utils import run_bass_kernel_spmd

F32 = mybir.dt.float32
BF16 = mybir.dt.bfloat16
U8 = mybir.dt.uint8
ALU = mybir.AluOpType
AF = mybir.ActivationFunctionType
AX = mybir.AxisListType

D = 1024
SEQ = 8192
NMETA = 16
LTOT = SEQ + NMETA
HALO = 32
OWN = 2048
CW = 544
NT = 4 * CW
KVW = 528
CSTW = 512 + 128 + 20 + 128 + 256
DEPTH = 2
EPS = 1e-6
N_IN = 6568
U0, Q0, C0, QI0, KI0, WI0, CB0, CC0, CU0, G0 = 0, 512, 1536, 1664, 1920, 1952, 1960, 2472, 2984, 3496
DFF = 2816
INDEX_SCALE = (8 ** -0.5) * (32 ** -0.5)
ATTN_SCALE = 128 ** -0.5
TOPK = 256
NEG = -2.0e4
MASKNEG = -30000.0
BIS_R = 16.0
BIS_IT = 16
TER_IT = 11
QUART = LTOT // 3
NGRP = 3
NSMALL = 72

CHUNKS = [(CW * i, CW, [(0, 32), (32, 512)]) for i in range(4)]
NMAX = 544
SUBS_ALL = [s_ for i in range(4) for s_ in ((CW * i, 32), (CW * i + 32, 512))]


def qblocks_of_chunk(ci):
    keh = NMETA + 512 * (4 * ci + 3)
    return [(0, 32, keh)] + [(32 + 128 * b_, 128, keh + 128 * (b_ + 1)) for b_ in range(4)]


def key_groups(KE):
    gs = []
    for g in range(NGRP):
        q0 = g * QUART
        q1 = min((g + 1) * QUART, KE)
        k = q0
        while k < q1:
            kn = min(512, q1 - k)
            gs.append((g, k - q0, k, kn))
            k += kn
    return gs


ALL_GROUPS = key_groups(LTOT)
NG = len(ALL_GROUPS)


nc_counts = {}

WSPEC = {
    "w_in": (1024, N_IN, 1024), "pool_w": (512, 128, 128), "w_a": (512, 1024, 1024), "w_b": (1024, 1024, 1024),
    "w_c": (512, 1024, 1024), "w_o": (1024, 1024, 1024), "w_gu": (1024, 2 * 2816, 1024), "w_dn": (2816, 1024, 256),
}
GROUPS4 = [[0, 1, 2, 3], [4, 5, 6, 7]]


def wpieces(name):
    K, N, pc = WSPEC[name]
    return [(c0, min(pc, N - c0)) for c0 in range(0, N, pc)]


class WMat:
    def __init__(self, K, pieces):
        self.K = K
        self.pieces = pieces

    def segs(self, col0, ncols, r0=0, kc=None):
        out = []
        kcc = self.K // 128 - r0 if kc is None else kc
        for (c0, pc, ap, key) in self.pieces:
            lo, hi = max(col0, c0), min(col0 + ncols, c0 + pc)
            if lo < hi:
                v = ap.rearrange("(kc p) n -> p kc n", p=128)
                out.append((lo - col0, hi - lo, v[:, r0:r0 + kcc, lo - c0:hi - c0], key))
        return out

    def seg1(self, col0, ncols):
        sg = self.segs(col0, ncols)
        assert len(sg) == 1
        return sg[0][2], sg[0][3]


class Prog:
    def __init__(self, nc, es):
        self.nc = nc
        self.es = es
        self.engs = {"pe": nc.tensor, "act": nc.scalar, "dve": nc.vector, "pool": nc.gpsimd, "sp": nc.sync}
        self.sems = {}
        self.cnt = {}
        self.isdma = {}
        self.waited = {}
        self.lastw = {}
        self.readers = {}
        for e in ("pe", "act", "dve", "pool"):
            self.new_sem(e, False)

    def new_sem(self, name, dma=True):
        self.sems[name] = self.es.enter_context(self.nc.semaphore("s_" + name))
        self.cnt[name] = 0
        self.isdma[name] = dma

    def _collect(self, r, w):
        deps = {}

        def add(tok):
            if tok is None:
                return
            s, v = tok
            if deps.get(s, 0) < v:
                deps[s] = v

        for k in r:
            add(self.lastw.get(k))
        for k in w:
            add(self.lastw.get(k))
            for s, v in self.readers.get(k, {}).items():
                add((s, v))
        return deps

    def _wait(self, e, deps):
        eng = self.engs[e]
        for s, v in deps.items():
            if self.isdma[s]:
                v = self.cnt[s]
            if self.waited.get((e, s), 0) < v:
                eng.wait_ge(self.sems[s], v)
                self.waited[(e, s)] = v

    def _record(self, r, w, tok):
        s, v = tok
        for k in r:
            d = self.readers.setdefault(k, {})
            if d.get(s, 0) < v:
                d[s] = v
        for k in w:
            self.lastw[k] = tok
            self.readers[k] = {}

    def op(self, e, fn, r=(), w=()):
        deps = self._collect(r, w)
        if e == "pe":
            deps.pop("pe", None)
        self._wait(e, deps)
        ins = fn(self.engs[e])
        self.cnt[e] += 1
        ins.then_inc(self.sems[e], 1)
        self._record(r, w, (e, self.cnt[e]))

    def dma(self, q, sem, fn, r=(), w=()):
        deps = self._collect(r, w)
        self._wait(q, deps)
        ins = fn(self.engs[q])
        self.cnt[sem] += 16
        ins.then_inc(self.sems[sem], 16)
        self._record(r, w, (sem, self.cnt[sem]))

    def barrier(self, engines=("pe", "act", "dve")):
        for e in engines:
            deps = {s: self.cnt[s] for s in ("pe", "act", "dve") if s != e and self.cnt[s] > 0}
            self._wait(e, deps)

    def wait_all(self, e, sems):
        self._wait(e, {s: self.cnt[s] for s in sems if self.cnt[s] > 0})


class _Stop(Exception):
    pass


def build(mode, dbg=False, stop=None):
    nc = bass.Bass("TRN2", target_bir_lowering=False)
    fused = mode == "FUSED"
    do_B = mode in ("L0", "L1")
    do_A = mode in ("A0", "L0")
    do_final = mode in ("L1", "FUSED")

    def din(name, shape):
        return nc.dram_tensor(name, shape, F32, kind="ExternalInput").ap()

    def dout(name, shape):
        return nc.dram_tensor(name, shape, F32, kind="ExternalOutput").ap()

    hin = din("hin", [D, NT])
    cst = din("cst", [128, CSTW + NG])
    small = din("small", [128, NSMALL])
    iota_in = nc.dram_tensor("iota16", [128, 512], mybir.dt.int16, kind="ExternalInput").ap()
    WL = [None, None]
    keys = None
    if fused:
        smallA = din("smallA", [128, NSMALL])
        shard_in, piece_in, piece_out, piece_bf = {}, {}, {}, {}
        for l in range(2):
            WL[l] = {}
            for name, (K, N, pc) in WSPEC.items():
                shard_in[(name, l)] = din("ws_%s_%d" % (name, l), [K // 4, N])
                pcs = []
                for pi, (c0, w_) in enumerate(wpieces(name)):
                    piece_in[(name, l, pi)] = nc.dram_tensor("pi_%s_%d_%d" % (name, l, pi), [K // 4, w_], F32)
                    piece_out[(name, l, pi)] = nc.dram_tensor("pg_%s_%d_%d" % (name, l, pi), [K, w_], F32)
                    piece_bf[(name, l, pi)] = nc.dram_tensor("pb_%s_%d_%d" % (name, l, pi), [K, w_], BF16)
                    pcs.append((c0, w_, piece_bf[(name, l, pi)].ap(), "pb_%s_%d_%d" % (name, l, pi)))
                WL[l][name] = WMat(K, pcs)
        kv_in = nc.dram_tensor("kv_in", [160, 4 * KVW], BF16)
        kv_all = nc.dram_tensor("kv_all", [640, 4 * KVW], BF16)
    else:
        lay = {"A0": 0, "L0": 0, "L1": 1}[mode]
        names = ["w_in"] if mode == "A0" else list(WSPEC.keys())
        WL[lay] = {nm: WMat(WSPEC[nm][0], [(0, WSPEC[nm][1], din(nm, [WSPEC[nm][0], WSPEC[nm][1]]), "ext")]) for nm in names}
        if mode == "L0":
            smallA = din("smallA", [128, NSMALL])
            WL[1] = {"w_in": WMat(1024, [(0, N_IN, din("w_inA", [D, N_IN]), "ext")])}
        elif mode == "A0":
            smallA = small
        if do_B:
            keys = din("keys", [160, LTOT])
        if do_A:
            kvout = dout("kvout", [160, NT])
    if mode == "L0":
        hout = dout("hout", [D, NT])
    if do_final:
        outT = dout("outT", [D, OWN])
    if dbg:
        dbg_attn = dout("dbg_attn", [D, NT]).rearrange("(k p) t -> p k t", p=128)
        dbg_m = dout("dbg_m", [D, NT]).rearrange("(k p) t -> p k t", p=128)
        dbg_hmid = dout("dbg_hmid", [D, NT]).rearrange("(k p) t -> p k t", p=128)

    with ExitStack() as es:
        P = Prog(nc, es)
        for s in ("ld", "ldp", "kv", "keys", "out", "w0", "w1", "w2", "ws", "dbgp", "bn0", "bn1", "bn2", "bn3", "bn4", "bn5", "cv0", "cv1", "cv2", "cv3", "cv4", "cv5"):
            P.new_sem(s)

        def sb(name, shape, dt):
            return es.enter_context(nc.sbuf_tensor(name, shape, dt))

        hT = sb("hT", [128, 8, NT], F32)
        cT = sb("cT", [128, LTOT], BF16)
        ki4 = sb("ki4", [128, QUART], BF16)
        attnT = sb("attnT", [128, 8, NMAX], BF16)
        cstt = sb("cstt", [128, CSTW - 640 + NG], F32)
        iota = sb("iota_sb", [128, 512], mybir.dt.int16)
        smt = sb("smt", [128, NSMALL], F32)
        smtA = sb("smtA", [128, NSMALL], F32)
        ones_bf = sb("ones_bf", [128, 128], BF16)
        I4 = sb("I4", [128, 512], BF16)
        ident_bf = I4[:, 0:128]
        wsm = sb("wsm", [128, 8, 168], BF16)
        wslots = [sb("wslot%d" % i, [128, 4096], BF16) for i in range(3)]
        tiny = sb("tiny", [128, 16], F32)
        junks = sb("junks", [128, 528], U8)
        junkA = sb("junkA", [128, 528], U8)
        wtok = sb("wtok", [128, 5, 8], F32)
        qrel = sb("qrel", [128, NG], F32)
        ARENA_W = 19736
        arena = sb("arena", [128, ARENA_W], F32)

        qpos = cstt[:, 0:20]
        cmask_all = cstt[:, 20:148]
        invcnt = cstt[:, 148:404]
        k0neg = cstt[:, CSTW - 640:CSTW - 640 + NG]

        psb = [es.enter_context(nc.psum_tensor("psb%d" % i, [128, 512], F32)) for i in range(7)]
        pst = es.enter_context(nc.psum_tensor("pst", [128, 1024], BF16))
        bank_rr = {"pool": [4, 5, 6], "i": 0}

        def next_bank():
            pool = bank_rr["pool"]
            b = pool[bank_rr["i"] % len(pool)]
            bank_rr["i"] += 1
            return b

        tr_rr = {"i": 0}

        def next_tr():
            i = tr_rr["i"] % 8
            tr_rr["i"] += 1
            return i

        def av_f32(off, shape):
            n = int(np.prod(shape))
            ap = arena[:, off:off + n]
            if len(shape) == 2:
                return ap.rearrange("p (a b) -> p a b", a=shape[0])
            return ap

        def av_bf(off, shape):
            n = int(np.prod(shape))
            assert n % 2 == 0
            ap = arena[:, off:off + n // 2].bitcast(BF16)
            if len(shape) == 2:
                return ap.rearrange("p (a b) -> p a b", a=shape[0])
            return ap

        wstate = {"i": 0}
        wq = "sp" if fused else "pool"

        def load_w(segs, kc, ncols):
            assert kc * ncols <= 4096
            i = wstate["i"] % 3
            wstate["i"] += 1
            view = wslots[i][:, 0:kc * ncols].rearrange("p (k n) -> p k n", k=kc)
            key = "wslot%d" % i
            for (off, n_, ap_, pkey) in segs:
                P.dma(wq, "w%d" % i, lambda e, off=off, n_=n_, ap_=ap_: e.dma_start(out=view[:, :, off:off + n_], in_=ap_), r=(pkey,), w=(key,))
            return view, key

        def wsrc(W, col0, ncols, r0=0, kc=None):
            return W.segs(col0, ncols, r0, kc)

        P.dma("sp", "ld", lambda e: e.dma_start(out=cstt[:], in_=cst[:, 640:CSTW + NG]), w=("cst",))
        P.dma("sp", "ld", lambda e: e.dma_start(out=iota[:], in_=iota_in), w=("cst",))
        for i4 in range(4):
            P.dma("pool", "ldp", lambda e, i4=i4: e.dma_start(out=I4[:, 128 * i4:128 * i4 + 128], in_=cst[:, 512:640]), w=("I4", "ident"))
        P.dma("sp", "ld", lambda e: e.dma_start(out=smt[:], in_=small), w=("small",))
        if do_A or fused:
            P.dma("sp", "ld", lambda e: e.dma_start(out=smtA[:], in_=smallA), w=("smallA",))
        hin_v = hin.rearrange("(k p) t -> p k t", p=128)
        for k in range(8):
            P.dma("sp", "ld", lambda e, k=k: e.dma_start(out=hT[:, k, :], in_=hin_v[:, k, :]), w=("h",))
        P.op("dve", lambda e: e.memset(ones_bf[:], 1.0), w=("ones",))

        def smv(t):
            return dict(g1=t[:, 0:8], g2=t[:, 8:16], kvg=t[:, 16:17], bg=t[:, 17:41], psc=t[:, 41:45],
                        cw=t[:, 45:57], fg=t[:, 57:65])

        SM = smv(smt)
        SMA = smv(smtA)

        def rms_norm_to(xn_view, t0, subs, gvec, gkey, sq_off):
            sq = av_bf(sq_off, (8, 512))
            rs = av_f32(sq_off + 2048, (512,))
            for (s0, n) in subs:
                b = next_bank()
                for k in range(8):
                    P.op("act", lambda e, k=k: e.activation(out=sq[:, k, 0:n], in_=hT[:, k, t0 + s0:t0 + s0 + n], func=AF.Square),
                         r=("h",), w=("sq%d" % k,))
                for k in range(8):
                    P.op("pe", lambda e, k=k: e.matmul(psb[b][:, 0:n], ones_bf[:], sq[:, k, 0:n], start=(k == 0), stop=(k == 7)),
                         r=("sq%d" % k, "ones"), w=("ps%d" % b,))
                P.op("act", lambda e: e.activation(out=rs[:, 0:n], in_=psb[b][:, 0:n], func=AF.Sqrt, scale=1.0 / D, bias=EPS),
                     r=("ps%d" % b,), w=("rs",))
                P.op("dve", lambda e: e.reciprocal(out=rs[:, 0:n], in_=rs[:, 0:n]), r=("rs",), w=("rs",))
                for k in range(8):
                    P.op("dve", lambda e, k=k: e.scalar_tensor_tensor(out=xn_view[:, k, s0:s0 + n], in0=hT[:, k, t0 + s0:t0 + s0 + n],
                                                                       scalar=gvec[:, k:k + 1], in1=rs[:, 0:n], op0=ALU.mult, op1=ALU.mult),
                         r=("h", "rs", gkey), w=("xn",))

        def stage_A(w_in_l, smx, smkey, subs_a=None, load=True):
            P.barrier()
            bank_rr["pool"] = [0, 1, 2, 3, 4, 5, 6]
            if subs_a is None:
                subs_a = SUBS_ALL
            if load:
                ap_c, k_c = w_in_l.seg1(C0, 128)
                ap_k, k_k = w_in_l.seg1(KI0, 32)
                P.dma(wq, "ws", lambda e: e.dma_start(out=wsm[:, :, 0:128], in_=ap_c), r=(k_c,), w=("wsm",))
                P.dma(wq, "ws", lambda e: e.dma_start(out=wsm[:, :, 128:160], in_=ap_k), r=(k_k,), w=("wsm",))
            xn = av_bf(0, (8, NMAX))
            c32 = av_f32(8192, (512,))
            csq = av_bf(8192 + 512, (512,))
            rs2 = av_f32(8192 + 1024, (512,))
            ki32 = av_f32(8192 + 1536, (512,))
            cbf = av_bf(8192 + 2048, (512,))
            kibf = av_bf(8192 + 2304, (512,))
            for (t0, n) in subs_a:
                rms_norm_to(xn, t0, [(0, n)], smx["g1"], smkey, 4352)
                b = next_bank()
                for k in range(8):
                    P.op("pe", lambda e, k=k: e.matmul(psb[b][:, 0:n], wsm[:, k, 0:128], xn[:, k, 0:n], start=(k == 0), stop=(k == 7)),
                         r=("xn", "wsm"), w=("ps%d" % b,))
                P.op("act", lambda e: e.activation(out=csq[:, 0:n], in_=psb[b][:, 0:n], func=AF.Square), r=("ps%d" % b,), w=("csq",))
                b2 = next_bank()
                P.op("pe", lambda e: e.matmul(psb[b2][:, 0:n], ones_bf[:], csq[:, 0:n], start=True, stop=True),
                     r=("csq", "ones"), w=("ps%d" % b2,))
                P.op("act", lambda e: e.activation(out=rs2[:, 0:n], in_=psb[b2][:, 0:n], func=AF.Sqrt, scale=1.0 / 128, bias=EPS),
                     r=("ps%d" % b2,), w=("rs2",))
                P.op("dve", lambda e: e.reciprocal(out=rs2[:, 0:n], in_=rs2[:, 0:n]), r=("rs2",), w=("rs2",))
                cdst = cbf if fused else c32
                kdst = kibf if fused else ki32
                P.op("dve", lambda e: e.scalar_tensor_tensor(out=cdst[:, 0:n], in0=psb[b][:, 0:n], scalar=smx["kvg"], in1=rs2[:, 0:n],
                                                             op0=ALU.mult, op1=ALU.mult), r=("ps%d" % b, "rs2", smkey), w=("c32",))
                b3 = next_bank()
                for k in range(8):
                    P.op("pe", lambda e, k=k: e.matmul(psb[b3][0:32, 0:n], wsm[:, k, 128:160], xn[:, k, 0:n], start=(k == 0), stop=(k == 7)),
                         r=("xn", "wsm"), w=("ps%d" % b3,))
                P.op("act", lambda e: e.copy(out=kdst[0:32, 0:n], in_=psb[b3][0:32, 0:n]), r=("ps%d" % b3,), w=("ki32",))
                if fused:
                    ich, cc_ = t0 // CW, t0 % CW
                    lo = max(cc_, 16)
                    d0, d1 = KVW * ich + lo - 16, KVW * ich + cc_ + n - 16
                    P.dma("sp", "kv", lambda e: e.dma_start(out=kv_in.ap()[0:128, d0:d1], in_=cdst[:, lo - cc_:n]), r=("c32",), w=("kvout",))
                    P.dma("sp", "kv", lambda e: e.dma_start(out=kv_in.ap()[128:160, d0:d1], in_=kdst[0:32, lo - cc_:n]), r=("ki32",), w=("kvout",))
                else:
                    P.dma("sp", "kv", lambda e: e.dma_start(out=kvout[0:128, t0:t0 + n], in_=c32[:, 0:n]), r=("c32",), w=("kvout",))
                    P.dma("sp", "kv", lambda e: e.dma_start(out=kvout[128:160, t0:t0 + n], in_=ki32[0:32, 0:n]), r=("ki32",), w=("kvout",))
            P.barrier()

        def stage_B(Wd, smx, smkey, hooks=None, skip_halo=False, post_chunk=None):
            w_in, pool_w, w_a, w_b, w_c, w_o, w_gu, w_dn = (Wd[k_] for k_ in ("w_in", "pool_w", "w_a", "w_b", "w_c", "w_o", "w_gu", "w_dn"))
            def load_keys():
              if fused:
                kva = kv_all.ap()

                def key_src(k0, k1):
                    segs = [(0, NMETA, 0, 0)] + [(NMETA + 512 * g_, 512, g_ % 4, KVW * (g_ // 4) + 16) for g_ in range(16)]
                    out = []
                    for (g0, gn, r, col) in segs:
                        lo, hi = max(k0, g0), min(k1, g0 + gn)
                        if lo < hi:
                            out.append((lo - k0, hi - lo, r, col + lo - g0))
                    return out

                for (off, n_, r, c0_) in key_src(0, LTOT):
                    P.dma("sp", "keys", lambda e, off=off, n_=n_, r=r, c0_=c0_: e.dma_start(out=cT[:, off:off + n_], in_=kva[160 * r:160 * r + 128, c0_:c0_ + n_]),
                          r=("kvall",), w=("cT",))
                for g in range(NGRP):
                    for (off, n_, r, c0_) in key_src(g * QUART, (g + 1) * QUART):
                        P.dma("sp", "keys", lambda e, off=off, n_=n_, r=r, c0_=c0_, g=g: e.dma_start(
                            out=ki4[32 * g:32 * g + 32, off:off + n_], in_=kva[160 * r + 128:160 * r + 160, c0_:c0_ + n_]), r=("kvall",), w=("ki4",))
              else:
                P.dma("pool", "keys", lambda e: e.dma_start(out=cT[:], in_=keys[0:128, :]), w=("cT",))
                for g in range(NGRP):
                    P.dma("pool", "keys", lambda e, g=g: e.dma_start(out=ki4[32 * g:32 * g + 32, :], in_=keys[128:160, g * QUART:(g + 1) * QUART]),
                          w=("ki4",))

            ap_w, k_w = w_in.seg1(WI0, 8)
            P.dma(wq, "ws", lambda e: e.dma_start(out=wsm[:, :, 160:168], in_=ap_w), r=(k_w,), w=("wsm",))

            for oi, ci in enumerate((3, 2, 1, 0)):
                t0, n, subs = CHUNKS[ci]
                qbs = qblocks_of_chunk(ci)
                cmask = cmask_all[:, 32 * ci:32 * ci + 32]
                P.barrier()
                bank_rr["pool"] = [0, 1, 2, 3, 4, 5, 6]
                xnA = av_bf(0, (8, NMAX))
                scores = av_f32(0, (LTOT,))
                q_t = av_bf(8208, (8, NMAX))
                qi_t = av_bf(10384, (8, NMAX))
                mbias = av_bf(12560, (LTOT,))
                rbufs = [av_f32(16664 + 512 * i, (512,)) for i in range(2)]
                pbufs = [av_bf(17688 + 512 * i, (8, 128)) for i in range(3)]
                cbats = [av_bf(19224 + 256 * i, (4, 128)) for i in range(2)]
                rcb = av_f32(17688, (512,))
                rms_norm_to(xnA, t0, subs, smx["g1"], smkey, 12560)
                for half in range(2):
                    wv, wk = load_w(wsrc(w_in, Q0 + 512 * half, 512), 8, 512)
                    for hh in range(4):
                        h = half * 4 + hh
                        for (s0, sn) in subs:
                            b = next_bank()
                            for k in range(8):
                                P.op("pe", lambda e, k=k: e.matmul(psb[b][:, 0:sn], wv[:, k, 128 * hh:128 * hh + 128], xnA[:, k, s0:s0 + sn],
                                                                     start=(k == 0), stop=(k == 7)), r=("xn", wk), w=("ps%d" % b,))
                            P.op("act", lambda e: e.copy(out=q_t[:, h, s0:s0 + sn], in_=psb[b][:, 0:sn]), r=("ps%d" % b,), w=("q",))
                for half in range(2):
                    i = wstate["i"] % 3
                    wstate["i"] += 1
                    key = "wslot%d" % i
                    view = wslots[i][:, 0:3072].rearrange("p (k h r d) -> p k h r d", k=8, h=4, r=3)
                    for hh in range(4):
                        h = half * 4 + hh
                        ap_q, k_q = w_in.seg1(QI0 + 32 * h, 32)
                        for r_ in range(3):
                            P.dma(wq, "w%d" % i, lambda e, hh=hh, r_=r_, ap_q=ap_q: e.dma_start(
                                out=view[:, :, hh, r_, :], in_=ap_q), r=(k_q,), w=(key,))
                    for hh in range(4):
                        h = half * 4 + hh
                        lw = wslots[i][:, 0:3072].rearrange("p (k h m) -> p k h m", k=8, h=4)
                        for (s0, sn) in subs:
                            b = next_bank()
                            for k in range(8):
                                P.op("pe", lambda e, k=k: e.matmul(psb[b][0:96, 0:sn], lw[:, k, hh, :], xnA[:, k, s0:s0 + sn],
                                                                     start=(k == 0), stop=(k == 7)), r=("xn", key), w=("ps%d" % b,))
                            P.op("act", lambda e: e.copy(out=qi_t[0:96, h, s0:s0 + sn], in_=psb[b][0:96, 0:sn]), r=("ps%d" % b,), w=("qi",))
                for qi_, (c0, nq, KE) in enumerate(qbs):
                    b = next_bank()
                    for k in range(8):
                        P.op("pe", lambda e, k=k: e.matmul(psb[b][0:nq, 0:8], xnA[:, k, c0:c0 + nq], wsm[:, k, 160:168],
                                                             start=(k == 0), stop=(k == 7)), r=("xn", "wsm"), w=("ps%d" % b,))
                    P.op("dve", lambda e: e.tensor_scalar(out=wtok[0:nq, qi_, :], in0=psb[b][0:nq, 0:8], scalar1=INDEX_SCALE, scalar2=None,
                                                          op0=ALU.mult), r=("ps%d" % b,), w=("wtok",))

                if oi == 0:
                    load_keys()
                if hooks and (oi, "p1") in hooks:
                    hooks[(oi, "p1")]()
                if stop == "p1":
                    raise _Stop()
                P.barrier()
                bank_rr["pool"] = [4, 5, 6]
                def idx_chain(qi_):
                    c0, nq, KE = qbs[qi_]
                    groups = key_groups(KE)
                    ng = len(groups)
                    qcol = 5 * ci + qi_
                    P.op("dve", lambda e: e.tensor_scalar(out=qrel[0:nq, 0:ng], in0=k0neg[0:nq, 0:ng], scalar1=qpos[0:nq, qcol:qcol + 1],
                                                          scalar2=None, op0=ALU.add), r=("cst",), w=("qrel",))
                    for gi, (g, j0, k0, kn) in enumerate(groups):
                        P.op("dve", lambda e: e.tensor_scalar(out=scores[0:nq, k0:k0 + kn], in0=iota[0:nq, 0:kn], scalar1=qrel[0:nq, gi:gi + 1],
                                                              scalar2=NEG, op0=ALU.is_gt, op1=ALU.mult), r=("qrel", "cst"), w=("sc%d" % gi,))
                        for h in range(8):
                            b = next_bank()
                            P.op("pe", lambda e: e.matmul(psb[b][0:nq, 0:kn], qi_t[32 * g:32 * g + 32, h, c0:c0 + nq], ki4[32 * g:32 * g + 32, j0:j0 + kn],
                                                          start=True, stop=True), r=("qi", "ki4"), w=("ps%d" % b,))
                            rb = rbufs[h % 2]
                            rk = "rb%d" % (h % 2)
                            P.op("act", lambda e: e.activation(out=rb[0:nq, 0:kn], in_=psb[b][0:nq, 0:kn], func=AF.Relu), r=("ps%d" % b,), w=(rk,))
                            P.op("dve", lambda e: e.scalar_tensor_tensor(out=scores[0:nq, k0:k0 + kn], in0=rb[0:nq, 0:kn], scalar=wtok[0:nq, qi_, h:h + 1],
                                                                         in1=scores[0:nq, k0:k0 + kn], op0=ALU.mult, op1=ALU.add),
                                 r=(rk, "wtok", "sc%d" % gi), w=("sc%d" % gi,))
                        yield

                def bisect(qi_):
                    c0, nq, KE = qbs[qi_]
                    ng = len(key_groups(KE))
                    sckeys = tuple("sc%d" % gi for gi in range(ng))
                    tb = 8 * (qi_ % 2)
                    tk = "t%d" % (qi_ % 2)
                    assert KE % 16 == 0
                    m = KE // 16
                    md = int(round(0.55 * m))
                    ma = m - md
                    jout = bass.AP(junks, 0, [[528, nq], [0, 16], [1, md]])
                    joutA = bass.AP(junkA, 0, [[528, nq], [0, 16], [1, ma]])
                    scd = scores[0:nq, 0:16 * md].rearrange("p (a b) -> p a b", a=16)
                    sca = scores[0:nq, 16 * md:KE].rearrange("p (a b) -> p a b", a=16)
                    PR, CA, CD, V, SG = (tiny[0:nq, tb + i:tb + i + 1] for i in (1, 2, 3, 4, 5))
                    P.op("dve", lambda e: e.reduce_max(out=tiny[0:nq, tb:tb + 1], in_=scores[0:nq, 0:KE], axis=AX.X), r=sckeys, w=(tk + "m",))
                    P.op("dve", lambda e: e.tensor_scalar(out=PR, in0=tiny[0:nq, tb:tb + 1], scalar1=-BIS_R / 2, scalar2=None, op0=ALU.add),
                         r=(tk + "m",), w=(tk + "p",))
                    vthr = 2.0 * TOPK - 16 * ma - 0.5
                    for it in range(BIS_IT):
                        P.op("act", lambda e: e.activation(out=joutA, in_=sca, func=AF.Sign, bias=PR, scale=-1.0, accum_out=CA),
                             r=sckeys + (tk + "p",), w=("junkA", tk + "ca"))
                        P.op("dve", lambda e: e.tensor_scalar(out=jout, in0=scd, scalar1=PR, scalar2=None, op0=ALU.is_ge, op1=ALU.add, accum_out=CD),
                             r=sckeys + (tk + "p",), w=("junk", tk + "cd"))
                        P.op("dve", lambda e: e.scalar_tensor_tensor(out=V, in0=CD, scalar=2.0, in1=CA, op0=ALU.mult, op1=ALU.subtract),
                             r=(tk + "cd", tk + "ca"), w=(tk + "v",))
                        if it < BIS_IT - 1:
                            c_next = BIS_R / 2 ** (it + 2)
                            P.op("dve", lambda e: e.tensor_scalar(out=SG, in0=V, scalar1=vthr, scalar2=2.0 * c_next, op0=ALU.is_ge, op1=ALU.mult),
                                 r=(tk + "v",), w=(tk + "s",))
                            P.op("dve", lambda e: e.scalar_tensor_tensor(out=PR, in0=SG, scalar=-c_next, in1=PR, op0=ALU.add, op1=ALU.add),
                                 r=(tk + "s", tk + "p"), w=(tk + "p",))
                        else:
                            c_last = BIS_R / 2 ** (it + 1)
                            P.op("dve", lambda e: e.tensor_scalar(out=SG, in0=V, scalar1=vthr, scalar2=-c_last, op0=ALU.is_lt, op1=ALU.mult),
                                 r=(tk + "v",), w=(tk + "s",))
                            P.op("dve", lambda e: e.tensor_tensor(out=tiny[0:nq, tb + 7:tb + 8], in0=SG, in1=PR, op=ALU.add),
                                 r=(tk + "s", tk + "p"), w=(tk + "t",))
                        yield

                def gen_mbias(qi_):
                    c0, nq, KE = qbs[qi_]
                    ng = len(key_groups(KE))
                    sckeys = tuple("sc%d" % gi for gi in range(ng))
                    tb = 8 * (qi_ % 2)
                    tk = "t%d" % (qi_ % 2)
                    P.op("dve", lambda e: e.tensor_scalar(out=mbias[0:nq, 0:KE], in0=scores[0:nq, 0:KE], scalar1=tiny[0:nq, tb + 7:tb + 8], scalar2=MASKNEG,
                                                          op0=ALU.is_lt, op1=ALU.mult), r=sckeys + (tk + "t",), w=("mbias",))

                def attention(qi_):
                    c0, nq, KE = qbs[qi_]
                    nkb = (KE + 127) // 128
                    I4v = I4[0:nq, :].rearrange("p (h t) -> p h t", h=4)[:, :, 0:nq]

                    def pv_ops(kb):
                        k0 = kb * 128
                        kn = min(128, KE - k0)
                        pb = pbufs[kb % 3]
                        cb_ = cbats[(kb // 4) % 2][:, kb % 4, :]
                        ck = "cbat%d" % ((kb // 4) % 2)
                        for hf in range(2):
                            pk = "pb%dh%d" % (kb % 3, hf)
                            P.op("pe", lambda e: e.matmul(psb[hf][:, 0:4 * nq].rearrange("p (h t) -> p h t", h=4), cb_[0:kn, :], pb[0:kn, 4 * hf:4 * hf + 4, 0:nq],
                                                          start=(kb == 0), stop=(kb == nkb - 1)), r=(pk, ck), w=("ps%d" % hf,))
                            P.op("pe", lambda e: e.matmul(psb[2 + hf][:, 0:4 * nq].rearrange("p (h t) -> p h t", h=4), ones_bf[0:kn, :], pb[0:kn, 4 * hf:4 * hf + 4, 0:nq],
                                                          start=(kb == 0), stop=(kb == nkb - 1)), r=(pk, "ones"), w=("ps%d" % (2 + hf),))

                    for kb in range(nkb):
                        k0 = kb * 128
                        kn = min(128, KE - k0)
                        if kb % 4 == 0:
                            par = (kb // 4) % 2
                            cbat = cbats[par]
                            ck = "cbat%d" % par
                            nb_ = min(4, nkb - kb)
                            for i_ in range(nb_):
                                kk0 = (kb + i_) * 128
                                kkn = min(128, KE - kk0)
                                P.op("pe", lambda e: e.transpose(pst[0:kkn, 512 * par + 128 * i_:512 * par + 128 * i_ + 128], cT[:, kk0:kk0 + kkn], ident_bf),
                                     r=("cT", "ident"), w=("pt",))
                            lastn = min(128, KE - (kb + nb_ - 1) * 128)
                            nfull = nb_ if lastn == 128 else nb_ - 1
                            if nfull > 0:
                                P.op("act", lambda e: e.copy(out=cbat[:, 0:nfull, :], in_=pst[:, 512 * par:512 * par + 128 * nfull].rearrange("p (a b) -> p a b", a=nfull)),
                                     r=("pt",), w=(ck,))
                            if nfull < nb_:
                                P.op("act", lambda e: e.copy(out=cbat[0:lastn, nfull, :], in_=pst[0:lastn, 512 * par + 128 * nfull:512 * par + 128 * nfull + 128]),
                                     r=("pt",), w=(ck,))
                        pb = pbufs[kb % 3]
                        for hf in range(2):
                            b = next_bank()
                            pk = "pb%dh%d" % (kb % 3, hf)
                            P.op("pe", lambda e: e.matmul(psb[b][0:kn, 0:4 * nq].rearrange("p (h t) -> p h t", h=4), cT[:, k0:k0 + kn],
                                                          q_t[:, 4 * hf:4 * hf + 4, c0:c0 + nq], start=True, stop=False), r=("cT", "q"), w=("ps%d" % b,))
                            P.op("pe", lambda e: e.matmul(psb[b][0:kn, 0:4 * nq].rearrange("p (h t) -> p h t", h=4), mbias[0:nq, k0:k0 + kn],
                                                          I4v, start=False, stop=True), r=("mbias", "I4"), w=("ps%d" % b,))
                            P.op("act", lambda e: e.activation(out=pb[0:kn, 4 * hf:4 * hf + 4, 0:nq],
                                                               in_=psb[b][0:kn, 0:4 * nq].rearrange("p (h t) -> p h t", h=4), func=AF.Exp, scale=ATTN_SCALE),
                                 r=("ps%d" % b,), w=(pk,))
                        if kb >= 1:
                            pv_ops(kb - 1)
                        yield
                    pv_ops(nkb - 1)
                    yield

                def normalize(qi_):
                    c0, nq, KE = qbs[qi_]
                    for hf in range(2):
                        P.op("dve", lambda e: e.reciprocal(out=rcb[:, 0:4 * nq], in_=psb[2 + hf][:, 0:4 * nq]), r=("ps%d" % (2 + hf),), w=("pb0h0", "pb0h1"))
                        P.op("dve", lambda e: e.tensor_tensor(out=attnT[:, 4 * hf:4 * hf + 4, c0:c0 + nq], in0=psb[hf][:, 0:4 * nq].rearrange("p (h t) -> p h t", h=4),
                                                              in1=rcb[:, 0:4 * nq].rearrange("p (h t) -> p h t", h=4), op=ALU.mult),
                             r=("ps%d" % hf, "pb0h0", "pb0h1"), w=("attnT",))

                def run(gen):
                    for _ in gen:
                        pass

                def side(qi_):
                    yield from idx_chain(qi_)
                    yield from bisect(qi_)

                nqb = len(qbs)
                q_first = 1 if skip_halo else 0
                run(side(q_first))
                gen_mbias(q_first)
                for qi_ in range(q_first, nqb):
                    if qi_ + 1 < nqb:
                        nside = len(key_groups(qbs[qi_ + 1][2])) + BIS_IT
                        S = side(qi_ + 1)
                        nA = (qbs[qi_][2] + 127) // 128 + 1
                        done = 0
                        for i_, _ in enumerate(attention(qi_)):
                            want = min(nside, ((i_ + 1) * nside) // (nA - 4 if nA > 8 else nA))
                            while done < want:
                                next(S, None)
                                done += 1
                        run(S)
                    else:
                        run(attention(qi_))
                    normalize(qi_)
                    if qi_ + 1 < nqb:
                        gen_mbias(qi_ + 1)

                P.barrier()
                if dbg:
                    for k in range(8):
                        P.dma("pool", "dbgp", lambda e, k=k: e.dma_start(out=dbg_attn[:, k, t0:t0 + n], in_=attnT[:, k, 0:n]), r=("attnT",), w=("dbgo",))
                if stop == "p2":
                    raise _Stop()
                bank_rr["pool"] = [0, 1, 2, 3, 4, 5, 6]
                xn = av_bf(0, (8, NMAX))
                m32 = av_f32(2176, (8, NMAX))
                mb = av_bf(6528, (8, NMAX))
                pa = av_bf(8704, (4, NMAX))
                pm = av_bf(9792, (4, NMAX))
                vc = av_bf(10880, (4, NMAX))
                ub = av_f32(11968, (16 + NMAX,))
                sA = av_f32(12528, (16 + NMAX,))
                sB = av_f32(13088, (16 + NMAX,))
                ccs = av_f32(13648, (NMAX,))
                vb = av_f32(14192, (2 + NMAX,))
                yb = av_f32(14738, (NMAX,))
                sg = av_f32(15282, (512,))
                tm = av_f32(15794, (512,))
                rms_norm_to(xn, t0, subs, smx["g1"], smkey, 12528)
                wv, wk = load_w(wsrc(w_in, U0, 512), 8, 512)
                pwv, pwk = load_w(wsrc(pool_w, 0, 128), 4, 128)
                for g in range(4):
                    wdw = 2 ** (g + 1)
                    P.op("dve", lambda e: e.memset(ub[:, 0:16], 0.0), w=("ub",))
                    for (s0, sn) in subs:
                        b = next_bank()
                        for k in range(8):
                            P.op("pe", lambda e, k=k: e.matmul(psb[b][:, 0:sn], wv[:, k, 128 * g:128 * g + 128], xn[:, k, s0:s0 + sn],
                                                                 start=(k == 0), stop=(k == 7)), r=("xn", wk), w=("ps%d" % b,))
                        P.op("act", lambda e: e.copy(out=ub[:, 16 + s0:16 + s0 + sn], in_=psb[b][:, 0:sn]), r=("ps%d" % b,), w=("ub",))
                    P.op("dve", lambda e: e.tensor_tensor(out=ub[:, 16:48], in0=ub[:, 16:48], in1=cmask, op=ALU.mult), r=("ub", "cst"), w=("ub",))
                    src, skey = ub, "ub"
                    tgl = [(sA, "sA"), (sB, "sB")]
                    W_ = 16 + n
                    sh = 1
                    for step in range(g + 1):
                        dst, dkey = tgl[step % 2]
                        P.op("dve", lambda e, src=src, dst=dst, sh=sh: e.tensor_tensor(out=dst[:, sh:W_], in0=src[:, sh:W_], in1=src[:, 0:W_ - sh], op=ALU.add),
                             r=(skey,), w=(dkey,))
                        if sh > 0:
                            P.op("dve", lambda e, src=src, dst=dst, sh=sh: e.tensor_copy(out=dst[:, 0:sh], in_=src[:, 0:sh]), r=(skey,), w=(dkey,))
                        src, skey = dst, dkey
                        sh *= 2
                    P.op("dve", lambda e, src=src: e.scalar_tensor_tensor(out=pa[:, g, 0:n], in0=src[:, 16:16 + n], scalar=1.0 / wdw, in1=ub[:, 16:16 + n],
                                                                          op0=ALU.mult, op1=ALU.subtract), r=(skey, "ub"), w=("pa",))
                    if ci == 0:
                        P.op("dve", lambda e, src=src: e.tensor_tensor(out=tm[:, 0:64], in0=src[:, 16:80], in1=invcnt[:, 64 * g:64 * g + 64], op=ALU.mult),
                             r=(skey, "cst"), w=("tm",))
                        P.op("dve", lambda e: e.tensor_tensor(out=pa[:, g, 0:64], in0=tm[:, 0:64], in1=ub[:, 16:80], op=ALU.subtract),
                             r=("tm", "ub"), w=("pa",))
                    for (s0, sn) in subs:
                        b = next_bank()
                        P.op("pe", lambda e: e.matmul(psb[b][:, 0:sn], pwv[:, g, :], pa[:, g, s0:s0 + sn], start=True, stop=True),
                             r=("pa", pwk), w=("ps%d" % b,))
                        P.op("dve", lambda e: e.tensor_scalar(out=pm[:, g, s0:s0 + sn], in0=psb[b][:, 0:sn], scalar1=smx["psc"][:, g:g + 1], scalar2=None,
                                                              op0=ALU.mult), r=("ps%d" % b, smkey), w=("pm",))

                def merge_pass(bi, ywt, ykc, ysrc, ykey, first, last):
                    for half in range(2):
                        gv, gk = load_w(wsrc(w_in, G0 + 1024 * bi + 512 * half, 512), 8, 512)
                        yv, yk = load_w(wsrc(ywt, 512 * half, 512), ykc, 512)
                        for jj in range(4):
                            j = 4 * half + jj
                            for (s0, sn) in subs:
                                bg_ = next_bank()
                                for k in range(8):
                                    P.op("pe", lambda e, k=k: e.matmul(psb[bg_][:, 0:sn], gv[:, k, 128 * jj:128 * jj + 128], xn[:, k, s0:s0 + sn],
                                                                         start=(k == 0), stop=(k == 7)), r=("xn", gk), w=("ps%d" % bg_,))
                                P.op("act", lambda e: e.activation(out=sg[:, 0:sn], in_=psb[bg_][:, 0:sn], func=AF.Sigmoid,
                                                                   bias=smx["bg"][:, 8 * bi + j:8 * bi + j + 1], scale=1.0),
                                     r=("ps%d" % bg_, smkey), w=("sg",))
                                by = next_bank()
                                for k in range(ykc):
                                    P.op("pe", lambda e, k=k: e.matmul(psb[by][:, 0:sn], yv[:, k, 128 * jj:128 * jj + 128], ysrc[:, k, s0:s0 + sn],
                                                                         start=(k == 0), stop=(k == ykc - 1)), r=(ykey, yk), w=("ps%d" % by,))
                                if first:
                                    P.op("dve", lambda e: e.tensor_tensor(out=m32[:, j, s0:s0 + sn], in0=psb[by][:, 0:sn], in1=sg[:, 0:sn], op=ALU.mult),
                                         r=("ps%d" % by, "sg"), w=("m32",))
                                else:
                                    P.op("dve", lambda e: e.tensor_tensor(out=tm[:, 0:sn], in0=psb[by][:, 0:sn], in1=sg[:, 0:sn], op=ALU.mult),
                                         r=("ps%d" % by, "sg"), w=("tm",))
                                    if last:
                                        P.op("dve", lambda e: e.tensor_tensor(out=mb[:, j, s0:s0 + sn], in0=tm[:, 0:sn], in1=m32[:, j, s0:s0 + sn], op=ALU.add),
                                             r=("tm", "m32"), w=("mb",))
                                    else:
                                        P.op("dve", lambda e: e.tensor_tensor(out=m32[:, j, s0:s0 + sn], in0=tm[:, 0:sn], in1=m32[:, j, s0:s0 + sn], op=ALU.add),
                                             r=("tm", "m32"), w=("m32",))

                merge_pass(0, w_a, 4, pm, "pm", True, False)

                wcc, kcc = load_w(wsrc(w_in, CC0, 512), 8, 512)
                wcu, kcu = load_w(wsrc(w_in, CU0, 512), 8, 512)
                wcb, kcb = load_w(wsrc(w_in, CB0, 512), 8, 512)
                for j in range(4):
                    for (s0, sn) in subs:
                        b = next_bank()
                        for k in range(8):
                            P.op("pe", lambda e, k=k: e.matmul(psb[b][:, 0:sn], wcc[:, k, 128 * j:128 * j + 128], xn[:, k, s0:s0 + sn],
                                                                 start=(k == 0), stop=(k == 7)), r=("xn", kcc), w=("ps%d" % b,))
                        P.op("act", lambda e: e.copy(out=ccs[:, s0:s0 + sn], in_=psb[b][:, 0:sn]), r=("ps%d" % b,), w=("ccs",))
                    P.op("dve", lambda e: e.memset(vb[:, 0:2], 0.0), w=("vb",))
                    for (s0, sn) in subs:
                        b = next_bank()
                        for k in range(8):
                            P.op("pe", lambda e, k=k: e.matmul(psb[b][:, 0:sn], wcu[:, k, 128 * j:128 * j + 128], xn[:, k, s0:s0 + sn],
                                                                 start=(k == 0), stop=(k == 7)), r=("xn", kcu), w=("ps%d" % b,))
                        P.op("dve", lambda e: e.tensor_tensor(out=vb[:, 2 + s0:2 + s0 + sn], in0=psb[b][:, 0:sn], in1=ccs[:, s0:s0 + sn], op=ALU.mult),
                             r=("ps%d" % b, "ccs"), w=("vb",))
                    P.op("dve", lambda e: e.tensor_tensor(out=vb[:, 2:34], in0=vb[:, 2:34], in1=cmask, op=ALU.mult), r=("vb", "cst"), w=("vb",))
                    cw = smx["cw"]
                    P.op("dve", lambda e: e.tensor_scalar(out=yb[:, 0:n], in0=vb[:, 2:2 + n], scalar1=cw[:, 3 * j + 2:3 * j + 3], scalar2=None, op0=ALU.mult),
                         r=("vb", smkey), w=("yb",))
                    P.op("dve", lambda e: e.scalar_tensor_tensor(out=yb[:, 0:n], in0=vb[:, 1:1 + n], scalar=cw[:, 3 * j + 1:3 * j + 2], in1=yb[:, 0:n],
                                                                 op0=ALU.mult, op1=ALU.add), r=("vb", "yb", smkey), w=("yb",))
                    P.op("dve", lambda e: e.scalar_tensor_tensor(out=yb[:, 0:n], in0=vb[:, 0:n], scalar=cw[:, 3 * j:3 * j + 1], in1=yb[:, 0:n],
                                                                 op0=ALU.mult, op1=ALU.add), r=("vb", "yb", smkey), w=("yb",))
                    for (s0, sn) in subs:
                        b = next_bank()
                        for k in range(8):
                            P.op("pe", lambda e, k=k: e.matmul(psb[b][:, 0:sn], wcb[:, k, 128 * j:128 * j + 128], xn[:, k, s0:s0 + sn],
                                                                 start=(k == 0), stop=(k == 7)), r=("xn", kcb), w=("ps%d" % b,))
                        P.op("dve", lambda e: e.tensor_tensor(out=vc[:, j, s0:s0 + sn], in0=psb[b][:, 0:sn], in1=yb[:, s0:s0 + sn], op=ALU.mult),
                             r=("ps%d" % b, "yb"), w=("vc",))
                merge_pass(2, w_c, 4, vc, "vc", False, False)
                merge_pass(1, w_b, 8, attnT, "attnT", False, True)

                for half in range(2):
                    wv, wk = load_w(wsrc(w_o, 512 * half, 512), 8, 512)
                    for jj in range(4):
                        j = 4 * half + jj
                        for (s0, sn) in subs:
                            b = next_bank()
                            for k in range(8):
                                P.op("pe", lambda e, k=k: e.matmul(psb[b][:, 0:sn], wv[:, k, 128 * jj:128 * jj + 128], mb[:, k, s0:s0 + sn],
                                                                     start=(k == 0), stop=(k == 7)), r=("mb", wk), w=("ps%d" % b,))
                            P.op("dve", lambda e: e.tensor_tensor(out=hT[:, j, t0 + s0:t0 + s0 + sn], in0=psb[b][:, 0:sn], in1=hT[:, j, t0 + s0:t0 + s0 + sn], op=ALU.add),
                                 r=("ps%d" % b, "h"), w=("h",))

                P.barrier()
                if dbg:
                    for k in range(8):
                        P.dma("pool", "dbgp", lambda e, k=k: e.dma_start(out=dbg_m[:, k, t0:t0 + n], in_=mb[:, k, 0:n]), r=("mb",), w=("dbgo",))
                        P.dma("sp", "out", lambda e, k=k: e.dma_start(out=dbg_hmid[:, k, t0:t0 + n], in_=hT[:, k, t0:t0 + n]), r=("h",), w=("dbgo",))
                    P.wait_all("act", ("out", "dbgp"))
                if stop == "p3":
                    raise _Stop()
                hn = av_bf(0, (8, NMAX))
                hid = av_bf(2176, (11, NMAX))
                sgf = av_f32(5168, (512,))
                rms_norm_to(hn, t0, subs, smx["g2"], smkey, 5680)
                for fh in range(2):
                    cbase = 1408 * fh
                    for (tc0, tcn) in ((0, 512), (512, 512), (1024, 384)):
                        gv, gk = load_w(wsrc(w_gu, cbase + tc0, tcn), 8, tcn)
                        uv, uk = load_w(wsrc(w_gu, DFF + cbase + tc0, tcn), 8, tcn)
                        for jj in range(tcn // 128):
                            jh = (tc0 // 128) + jj
                            for (s0, sn) in subs:
                                bg_ = next_bank()
                                for k in range(8):
                                    P.op("pe", lambda e, k=k: e.matmul(psb[bg_][:, 0:sn], gv[:, k, 128 * jj:128 * jj + 128], hn[:, k, s0:s0 + sn],
                                                                         start=(k == 0), stop=(k == 7)), r=("xn", gk), w=("ps%d" % bg_,))
                                P.op("act", lambda e: e.activation(out=sgf[:, 0:sn], in_=psb[bg_][:, 0:sn], func=AF.Silu), r=("ps%d" % bg_,), w=("sgf",))
                                bu = next_bank()
                                for k in range(8):
                                    P.op("pe", lambda e, k=k: e.matmul(psb[bu][:, 0:sn], uv[:, k, 128 * jj:128 * jj + 128], hn[:, k, s0:s0 + sn],
                                                                         start=(k == 0), stop=(k == 7)), r=("xn", uk), w=("ps%d" % bu,))
                                P.op("dve", lambda e: e.tensor_tensor(out=hid[:, jh, s0:s0 + sn], in0=psb[bu][:, 0:sn], in1=sgf[:, 0:sn], op=ALU.mult),
                                     r=("ps%d" % bu, "sgf"), w=("hid",))
                    for oc2 in range(4):
                        dv, dk = load_w(wsrc(w_dn, 256 * oc2, 256, r0=11 * fh, kc=11), 11, 256)
                        for o in range(2):
                            j = 2 * oc2 + o
                            for (s0, sn) in subs:
                                b = next_bank()
                                for k in range(11):
                                    P.op("pe", lambda e, k=k: e.matmul(psb[b][:, 0:sn], dv[:, k, 128 * o:128 * o + 128], hid[:, k, s0:s0 + sn],
                                                                         start=(k == 0), stop=(k == 10)), r=("hid", dk), w=("ps%d" % b,))
                                P.op("dve", lambda e: e.tensor_tensor(out=hT[:, j, t0 + s0:t0 + s0 + sn], in0=psb[b][:, 0:sn], in1=hT[:, j, t0 + s0:t0 + s0 + sn], op=ALU.add),
                                     r=("ps%d" % b, "h"), w=("h",))
                if stop == "c0":
                    raise _Stop()
                if post_chunk:
                    post_chunk(oi, ci)
            P.barrier()

        def cc_allgather(src_t, dst_t, rkeys, wkeys, tag):
            P.new_sem(tag, False)
            deps = P._collect(rkeys, wkeys)
            P._wait("pool", deps)
            ins = nc.gpsimd.collective_compute("AllGather", ALU.bypass, replica_groups=GROUPS4, ins=[src_t.ap().opt()], outs=[dst_t.ap().opt()])
            ins.then_inc(P.sems[tag])
            P.cnt[tag] = 1
            P._record(rkeys, wkeys, (tag, 1))

        stopped = False
        if fused:
            order = [("w_in", 1), ("w_in", 0), ("pool_w", 0), ("w_in", 3), ("w_a", 0), ("w_in", 2), ("w_in", 4), ("w_in", 5), ("w_c", 0),
                     ("w_in", 6), ("w_b", 0), ("w_o", 0)] + [("w_gu", i) for i in range(6)] + [("w_dn", i) for i in range(4)]
            assert len(order) == sum(len(wpieces(nm)) for nm in WSPEC)
            cvstate = {"sem": "cv0", "n": 0}

            def bounce(l, nm, pi):
                c0, w_ = wpieces(nm)[pi]
                P.dma("act", cvstate["sem"].replace("cv", "bn"), lambda e: e.dma_start(out=piece_in[(nm, l, pi)].ap(), in_=shard_in[(nm, l)][:, c0:c0 + w_]),
                      w=("pi_%s_%d_%d" % (nm, l, pi),))

            def cc_piece(l, nm, pi):
                cc_allgather(piece_in[(nm, l, pi)], piece_out[(nm, l, pi)], ("pi_%s_%d_%d" % (nm, l, pi),), ("pg_%s_%d_%d" % (nm, l, pi),),
                             "cc_%s_%d_%d" % (nm, l, pi))
                P.dma("pool", cvstate["sem"], lambda e: e.dma_start(out=piece_bf[(nm, l, pi)].ap(), in_=piece_out[(nm, l, pi)].ap()),
                      r=("pg_%s_%d_%d" % (nm, l, pi),), w=("pb_%s_%d_%d" % (nm, l, pi),))

            def cc_batch(sem, items):
                cvstate["sem"] = sem
                for (l_, nm_, pi_) in items:
                    bounce(l_, nm_, pi_)
                for (l_, nm_, pi_) in items:
                    cc_piece(l_, nm_, pi_)

            rest0 = [x for x in order if x not in (("w_in", 1), ("w_in", 0))]
            hooks0 = {(0, "p1"): lambda: cc_batch("cv2", [(0,) + x for x in rest0] + [(1,) + order[0]]),
                      (1, "p1"): lambda: cc_batch("cv3", [(1,) + x for x in order[1:8]]),
                      (2, "p1"): lambda: cc_batch("cv4", [(1,) + x for x in order[8:15]]),
                      (3, "p1"): lambda: cc_batch("cv5", [(1,) + x for x in order[15:22]])}
            cc_batch("cv0", [(0, "w_in", 1)])
            cc_batch("cv1", [(0, "w_in", 0)])
            stage_A(WL[0]["w_in"], SM, "small")
            cc_allgather(kv_in, kv_all, ("kvout",), ("kvall",), "cckv0")
            stage_B(WL[0], SM, "small", hooks0,
                    post_chunk=lambda oi, ci: stage_A(WL[1]["w_in"], SMA, "smallA", subs_a=[(CW * ci, 32), (CW * ci + 32, 512)], load=(oi == 0)))
            cc_allgather(kv_in, kv_all, ("kvout",), ("kvall",), "cckv1")
            stage_B(WL[1], SMA, "smallA", skip_halo=True)
        elif mode == "A0":
            stage_A(WL[0]["w_in"], SM, "small")
        else:
            try:
                stage_B(WL[lay], SM, "small")
            except _Stop:
                stopped = True
            if mode == "L0":
                if not stopped:
                    stage_A(WL[1]["w_in"], SMA, "smallA")
                for k in range(8):
                    P.dma("sp", "out", lambda e, k=k: e.dma_start(out=hout.rearrange("(k p) t -> p k t", p=128)[:, k, :], in_=hT[:, k, :]),
                          r=("h",), w=("hout",))
        if True:
            if do_final:
                P.barrier()
                bank_rr["pool"] = [0, 1, 2, 3, 4, 5, 6]
                sq = av_bf(0, (8, 512))
                rs = av_f32(2048, (512,))
                ob = av_f32(2560, (8, 512))
                outv = outT.rearrange("(k p) t -> p k t", p=128)
                for si in range(4):
                    t0 = CW * si + 32
                    n = 512
                    b = next_bank()
                    for k in range(8):
                        P.op("act", lambda e, k=k: e.activation(out=sq[:, k, 0:n], in_=hT[:, k, t0:t0 + n], func=AF.Square), r=("h",), w=("sq%d" % k,))
                    for k in range(8):
                        P.op("pe", lambda e, k=k: e.matmul(psb[b][:, 0:n], ones_bf[:], sq[:, k, 0:n], start=(k == 0), stop=(k == 7)),
                             r=("sq%d" % k, "ones"), w=("ps%d" % b,))
                    P.op("act", lambda e: e.activation(out=rs[:, 0:n], in_=psb[b][:, 0:n], func=AF.Sqrt, scale=1.0 / D, bias=EPS), r=("ps%d" % b,), w=("rs",))
                    P.op("dve", lambda e: e.reciprocal(out=rs[:, 0:n], in_=rs[:, 0:n]), r=("rs",), w=("rs",))
                    for k in range(8):
                        P.op("dve", lambda e, k=k: e.scalar_tensor_tensor(out=ob[:, k, 0:n], in0=hT[:, k, t0:t0 + n], scalar=SM["fg"][:, k:k + 1], in1=rs[:, 0:n],
                                                                           op0=ALU.mult, op1=ALU.mult), r=("h", "rs", "small"), w=("ob",))
                    for k in range(8):
                        P.dma("sp", "out", lambda e, k=k: e.dma_start(out=outv[:, k, 512 * si:512 * si + 512], in_=ob[:, k, 0:n]), r=("ob",), w=("outT",))
        P.wait_all("sp", ("out", "kv", "dbgp"))
        P.wait_all("sp", ("pe", "act", "dve"))
    nc_counts.update(P.cnt)
    return nc


def _core_positions(j):
    c = j % 4
    pos = np.zeros(NT, np.int64)
    for i in range(4):
        g = 4 * i + c
        pos[CW * i:CW * (i + 1)] = NMETA + 512 * g - HALO + np.arange(CW)
    return pos


def _const_pack(j):
    pos = _core_positions(j)
    cst = np.zeros((128, CSTW + NG), np.float32)
    cst[:, 0:512] = np.arange(512, dtype=np.float32)[None, :]
    cst[:, 512:640] = np.eye(128, dtype=np.float32)
    qp = np.zeros((128, 20), np.float32)
    for i in range(4):
        qp[0:32, 5 * i] = np.maximum(pos[CW * i:CW * i + 32], 0)
        for b_ in range(4):
            qp[:, 5 * i + 1 + b_] = np.maximum(pos[CW * i + 32 + 128 * b_:CW * i + 32 + 128 * (b_ + 1)], 0)
    cst[:, 640:660] = qp
    for i in range(4):
        cst[:, 660 + 32 * i:660 + 32 * (i + 1)] = (pos[CW * i:CW * i + 32] >= 0).astype(np.float32)[None, :]
    inv = np.zeros((4, 64), np.float32)
    for g in range(4):
        w = 2 ** (g + 1)
        p64 = pos[0:64]
        cnt = np.where(p64 >= 0, np.minimum(p64 + 1, w), w)
        inv[g] = 1.0 / cnt
    cst[:, 788:1044] = inv.reshape(1, 256)
    cst[:, CSTW:CSTW + NG] = -np.array([g[2] for g in ALL_GROUPS], np.float32)[None, :]
    return cst


def _small_pack(inp, l):
    sm = np.zeros((128, NSMALL), np.float32)
    sm[:, 0:8] = inp["norm1_g"][l].reshape(8, 128).T
    sm[:, 8:16] = inp["norm2_g"][l].reshape(8, 128).T
    sm[:, 16] = inp["kv_norm_g"][l]
    sm[:, 17:41] = inp["b_gate"][l].reshape(24, 128).T
    sm[:, 41:45] = inp["pool_scale"][l].reshape(4, 128).T
    cw = inp["conv_w"][l].reshape(3, 4, 128)
    sm[:, 45:57] = cw.transpose(2, 1, 0).reshape(128, 12)
    sm[:, 57:65] = inp["final_g"].reshape(8, 128).T
    return sm


def _layer_weights(inp, l):
    c = np.ascontiguousarray
    return dict(w_in=c(inp["w_in"][l]), pool_w=c(inp["pool_w"][l].reshape(512, 128)), w_a=c(inp["w_a"][l]), w_b=c(inp["w_b"][l]),
                w_c=c(inp["w_c"][l]), w_o=c(inp["w_o"][l]), w_gu=c(inp["w_gate_up"][l]), w_dn=c(inp["w_down"][l]))


def _gather_keys(kvs):
    out = []
    for b in range(2):
        parts = [kvs[4 * b][:, 16:NT]] + [kvs[4 * b + c][:, HALO:NT] for c in range(1, 4)]
        out.append(np.ascontiguousarray(np.concatenate(parts, axis=1)))
    return out


_NC_CACHE = {}


def _get_nc(mode):
    if mode not in _NC_CACHE:
        _NC_CACHE[mode] = build(mode)
    return _NC_CACHE[mode]


def kernel(**inp):
    inp = {k: np.asarray(v, dtype=np.float32) for k, v in inp.items()}
    x, meta = inp["x"], inp["meta"]
    cores = list(range(8))
    csts = [_const_pack(j) for j in cores]
    hins = []
    for j in cores:
        b = j // 4
        pos = _core_positions(j)
        hfull = np.concatenate([meta, x[b]], axis=0)
        xin = np.zeros((NT, D), np.float32)
        valid = pos >= 0
        xin[valid] = hfull[pos[valid]]
        hins.append(np.ascontiguousarray(xin.T))
    sm0, sm1 = _small_pack(inp, 0), _small_pack(inp, 1)
    WLs = [_layer_weights(inp, 0), _layer_weights(inp, 1)]
    in_maps = []
    for j in cores:
        r = j % 4
        m = dict(hin=hins[j], cst=csts[j], small=sm0, smallA=sm1, iota16=np.tile(np.arange(512, dtype=np.int16)[None, :], (128, 1)))
        for l in range(2):
            for name, (K, N, pc) in WSPEC.items():
                m["ws_%s_%d" % (name, l)] = np.ascontiguousarray(WLs[l][name][r * (K // 4):(r + 1) * (K // 4)])
        in_maps.append(m)
    res = run_bass_kernel_spmd(_get_nc("FUSED"), in_maps, core_ids=cores)
    out = np.zeros((2, SEQ, D), np.float32)
    for j in cores:
        b, c = j // 4, j % 4
        oT = res.results[j]["outT"]
        for i in range(4):
            g = 4 * i + c
            out[b, 512 * g:512 * (g + 1), :] = oT[:, 512 * i:512 * (i + 1)].T
    return out
```
